# Optimizing a Trainium2 kernel written in Bass

```python
import jax, jax.numpy as jnp
from jax import lax
import numpy as np

D_MODEL = 4096
BATCH = 2
SEQ = 4096
DEPTH = 2
DEC_BATCH = 16
DEC_SEQ = 32
PAST_LEN = 2048

CHUNK = 64
D_MIX = D_MODEL
GROUP_WIDTH = D_MIX // 4
A_WIDTH = GROUP_WIDTH
B_WIDTH = GROUP_WIDTH
C_WIDTH = GROUP_WIDTH
D_WIDTH = GROUP_WIDTH
A_CHUNK = 128
A_HEAD_DIM = 128
A_HEADS = A_WIDTH // A_HEAD_DIM
B_KERNEL = 3
C_KERNEL = 31
C_HEAD_DIM = 128
C_HEADS = C_WIDTH // C_HEAD_DIM
POOL_WINDOWS = (2, 4, 8, 16)
POOL_GROUP = D_WIDTH // len(POOL_WINDOWS)
POOL_HIST = max(POOL_WINDOWS) - 1
D_IN = 2 * A_WIDTH + 3 * B_WIDTH + 2 * C_WIDTH + D_WIDTH
D_FF = 4 * D_MODEL
EPS = 1e-6

kernel_name = 'hybrid_streaming_encoder_step'


def rmsnorm(x, g):
    xf = x.astype(jnp.float32)
    y = xf * lax.rsqrt(jnp.mean(xf * xf, axis=-1, keepdims=True) + EPS)
    return (y * g.astype(jnp.float32)).astype(x.dtype)


def causal_dwconv(x, hist, w):
    k = w.shape[0]
    xp = jnp.concatenate([hist.astype(x.dtype), x], axis=1)
    y = lax.conv_general_dilated(xp, w[:, None, :].astype(x.dtype), window_strides=(1,), padding='VALID',
                                 dimension_numbers=('NWC', 'WIO', 'NWC'), feature_group_count=x.shape[-1])
    return y, xp[:, -(k - 1):]


def chunk_spatial_gate(u, v, ws, bias):
    b, t, _ = v.shape
    tp = -(-t // A_CHUNK) * A_CHUNK
    vp = jnp.pad(v, ((0, 0), (0, tp - t), (0, 0))).reshape(b, tp // A_CHUNK, A_CHUNK, A_HEADS, A_HEAD_DIM)
    wm = ws.astype(v.dtype) * jnp.tril(jnp.ones((A_CHUNK, A_CHUNK), v.dtype))
    mix = jnp.einsum('hij,bcjhd->bcihd', wm, vp) + bias.T.astype(v.dtype)[None, None, :, :, None]
    return u * mix.reshape(b, tp, A_WIDTH)[:, :t]


def multiscale_pool(p, hist, w_grp, scale, pos0):
    b, t, _ = p.shape
    xp = jnp.concatenate([hist.astype(p.dtype), p], axis=1)
    xf = xp.astype(jnp.float32)
    cs = jnp.concatenate([jnp.zeros((b, 1, D_WIDTH), jnp.float32), jnp.cumsum(xf, axis=1)], axis=1)
    pos = pos0 + jnp.arange(t)
    outs = []
    for g, w in enumerate(POOL_WINDOWS):
        sl = slice(g * POOL_GROUP, (g + 1) * POOL_GROUP)
        s = cs[:, POOL_HIST + 1:POOL_HIST + 1 + t, sl] - cs[:, POOL_HIST + 1 - w:POOL_HIST + 1 - w + t, sl]
        cnt = jnp.minimum(pos + 1, w).astype(jnp.float32)
        outs.append(s / cnt[None, :, None])
    pooled = jnp.concatenate(outs, axis=-1)
    d = (pooled - p.astype(jnp.float32)).astype(p.dtype).reshape(b, t, len(POOL_WINDOWS), POOL_GROUP)
    y = jnp.einsum('btgc,gcd->btgd', d, w_grp.astype(p.dtype)).reshape(b, t, D_WIDTH) * scale.astype(p.dtype)
    return y, xp[:, -POOL_HIST:]


def mixer_block(h, hist_b, hist_c, hist_d, pos0, w_in, a_ws, a_b, b_conv, c_conv, c_conv_b, c_ln_g, c_ln_b,
                d_w, d_scale, w_out):
    b, t, _ = h.shape
    z = h @ w_in.astype(h.dtype)
    a_u, a_v, b_h, b_b, b_c, c_a, c_g, d_p = jnp.split(z, 8, axis=-1)
    a_u = jax.nn.gelu(a_u, approximate=False)
    a_v = jax.nn.gelu(a_v, approximate=False)
    ya = chunk_spatial_gate(a_u, a_v, a_ws, a_b)
    conv_b, new_hb = causal_dwconv(b_c * b_h, hist_b, b_conv)
    yb = b_b * conv_b
    glu = c_a * jax.nn.sigmoid(c_g)
    conv_c, new_hc = causal_dwconv(glu, hist_c, c_conv)
    cf = (conv_c + c_conv_b.astype(h.dtype)).astype(jnp.float32).reshape(b, t, C_HEADS, C_HEAD_DIM)
    mu = jnp.mean(cf, axis=-1, keepdims=True)
    var = jnp.mean(jnp.square(cf - mu), axis=-1, keepdims=True)
    cn = ((cf - mu) * lax.rsqrt(var + EPS)).reshape(b, t, C_WIDTH)
    yc = jax.nn.silu(cn * c_ln_g.astype(jnp.float32) + c_ln_b.astype(jnp.float32)).astype(h.dtype)
    yd, new_hd = multiscale_pool(d_p, hist_d, d_w, d_scale, pos0)
    mix = jnp.concatenate([ya, yb, yc, yd], axis=-1) @ w_out.astype(h.dtype)
    return mix, new_hb, new_hc, new_hd, a_v


def trunk(x, hist_b, hist_c, hist_d, pos0, g_mix, w_in, a_ws, a_b, b_conv, c_conv, c_conv_b, c_ln_g, c_ln_b,
          d_w, d_scale, w_out, g_ffn, w_ff1, w_ff2, g_final):
    nb, nc, nd, av = [], [], [], []
    for l in range(DEPTH):
        h = rmsnorm(x, g_mix[l])
        m, hb, hc, hd, v = mixer_block(h, hist_b[l], hist_c[l], hist_d[l], pos0, w_in[l], a_ws[l], a_b[l],
                                       b_conv[l], c_conv[l], c_conv_b[l], c_ln_g[l], c_ln_b[l], d_w[l],
                                       d_scale[l], w_out[l])
        x = x + m
        h = rmsnorm(x, g_ffn[l])
        x = x + jnp.square(jax.nn.relu(h @ w_ff1[l].astype(h.dtype))) @ w_ff2[l].astype(h.dtype)
        nb.append(hb)
        nc.append(hc)
        nd.append(hd)
        av.append(v)
    return rmsnorm(x, g_final), jnp.stack(nb), jnp.stack(nc), jnp.stack(nd), jnp.stack(av)


def setup_inputs(seed: int = 0) -> dict:
    key = jax.random.key(seed)
    ks = jax.random.split(key, 24)

    def nrm(k, shape, s=1.0):
        return jax.random.normal(k, shape, jnp.float32) * s

    return {
        'x_prompt': nrm(ks[0], (BATCH, SEQ, D_MODEL)),
        'x_sample': nrm(ks[1], (DEC_BATCH, DEC_SEQ, D_MODEL)),
        'state_conv_b': nrm(ks[2], (DEPTH, DEC_BATCH, B_KERNEL - 1, B_WIDTH)),
        'state_conv_c': nrm(ks[3], (DEPTH, DEC_BATCH, C_KERNEL - 1, C_WIDTH), 0.5),
        'state_pool': nrm(ks[4], (DEPTH, DEC_BATCH, POOL_HIST, D_WIDTH)),
        'g_mix': 1.0 + nrm(ks[5], (DEPTH, D_MODEL), 0.1),
        'w_in': nrm(ks[6], (DEPTH, D_MODEL, D_IN), D_MODEL ** -0.5),
        'a_ws': nrm(ks[7], (DEPTH, A_HEADS, A_CHUNK, A_CHUNK), A_CHUNK ** -0.5),
        'a_b': 1.0 + nrm(ks[8], (DEPTH, A_HEADS, A_CHUNK), 0.1),
        'b_conv': nrm(ks[9], (DEPTH, B_KERNEL, B_WIDTH), B_KERNEL ** -0.5),
        'c_conv': nrm(ks[10], (DEPTH, C_KERNEL, C_WIDTH), C_KERNEL ** -0.5),
        'c_conv_b': nrm(ks[11], (DEPTH, C_WIDTH), 0.02),
        'c_ln_g': 1.0 + nrm(ks[12], (DEPTH, C_WIDTH), 0.1),
        'c_ln_b': nrm(ks[13], (DEPTH, C_WIDTH), 0.02),
        'd_w': nrm(ks[14], (DEPTH, len(POOL_WINDOWS), POOL_GROUP, POOL_GROUP), POOL_GROUP ** -0.5),
        'd_scale': 1.0 + nrm(ks[15], (DEPTH, D_WIDTH), 0.1),
        'w_out': nrm(ks[16], (DEPTH, D_MIX, D_MODEL), D_MIX ** -0.5),
        'g_ffn': 1.0 + nrm(ks[17], (DEPTH, D_MODEL), 0.1),
        'w_ff1': nrm(ks[18], (DEPTH, D_MODEL, D_FF), D_MODEL ** -0.5),
        'w_ff2': nrm(ks[19], (DEPTH, D_FF, D_MODEL), D_FF ** -0.5),
        'g_final': 1.0 + nrm(ks[20], (D_MODEL,), 0.1),
    }


def reference(x_prompt, x_sample, state_conv_b, state_conv_c, state_pool, g_mix, w_in, a_ws, a_b, b_conv,
              c_conv, c_conv_b, c_ln_g, c_ln_b, d_w, d_scale, w_out, g_ffn, w_ff1, w_ff2, g_final):
    bp = x_prompt.shape[0]
    zb = jnp.zeros((DEPTH, bp, B_KERNEL - 1, B_WIDTH), x_prompt.dtype)
    zc = jnp.zeros((DEPTH, bp, C_KERNEL - 1, C_WIDTH), x_prompt.dtype)
    zd = jnp.zeros((DEPTH, bp, POOL_HIST, D_WIDTH), x_prompt.dtype)
    y_prompt, new_conv_b_prompt, new_conv_c_prompt, new_pool_prompt, _ = trunk(
        x_prompt, zb, zc, zd, 0, g_mix, w_in, a_ws, a_b, b_conv, c_conv, c_conv_b, c_ln_g, c_ln_b,
        d_w, d_scale, w_out, g_ffn, w_ff1, w_ff2, g_final)
    y_sample, new_conv_b_sample, new_conv_c_sample, new_pool_sample, new_a_v_sample = trunk(
        x_sample, state_conv_b, state_conv_c, state_pool, PAST_LEN, g_mix, w_in, a_ws, a_b, b_conv, c_conv,
        c_conv_b, c_ln_g, c_ln_b, d_w, d_scale, w_out, g_ffn, w_ff1, w_ff2, g_final)
    return (y_prompt, y_sample, new_conv_b_prompt, new_conv_c_prompt, new_pool_prompt,
            new_conv_b_sample, new_conv_c_sample, new_pool_sample, new_a_v_sample)
```

```python
import numpy as np
import concourse.bass as bass
import concourse.mybir as mybir
from concourse.bass_utils import run_bass_kernel_spmd

F32 = mybir.dt.float32
BF16 = mybir.dt.bfloat16
AF = mybir.ActivationFunctionType
ALU = mybir.AluOpType

P = 128
NCH = 32
DEPTH = 2
EPS = 1e-6
NTMP = 6
LPAD = 672
POOLW = (2, 4, 8, 16)


class Op:
    __slots__ = ("eng", "fn", "deps", "flagged", "val", "sem", "is_dma", "key")

    def __init__(self, eng, fn, is_dma=False):
        self.eng = eng
        self.fn = fn
        self.deps = ()
        self.flagged = False
        self.val = None
        self.sem = None
        self.is_dma = is_dma
        self.key = eng


class Rec:
    ENGS = ("pe", "act", "dve", "pool", "sp")

    def __init__(self):
        self.ops = {e: [] for e in self.ENGS}
        self.last_w = {}
        self.readers = {}
        self.nchan = 8
        self.chan_use = [0] * self.nchan
        self.chan_last = [None] * self.nchan
        self.chan_rr = 0
        self.named_chan = {}
        self.ndma = 0

    def add(self, eng, fn, reads=(), writes=(), dma=False, chan=None):
        op = Op(eng, fn, dma)
        deps = set()
        for r in reads:
            w = self.last_w.get(r)
            if w is not None:
                deps.add(w)
        for r in writes:
            w = self.last_w.get(r)
            if w is not None:
                deps.add(w)
            rd = self.readers.get(r)
            if rd:
                deps.update(rd.values())
        if dma:
            self.ndma += 1
            op.key = ("dma", self.ndma)
            if chan is None:
                c = self.chan_rr
                self.chan_rr = (self.chan_rr + 1) % self.nchan
                ckey = ("rr", c)
            else:
                ckey = ("named", chan)
            st = self.named_chan.setdefault(ckey, [0, None])
            if st[1] is not None:
                deps.add(st[1])
            st[0] += 1
            st[1] = op
            op.sem = ckey
            op.val = 16 * st[0]
            op.flagged = True
        if eng == "pe":
            deps = {d for d in deps if d.is_dma or d.eng != "pe"}
        deps.discard(op)
        for d in deps:
            d.flagged = True
        op.deps = deps
        for r in reads:
            self.readers.setdefault(r, {})[op.key] = op
        for r in writes:
            self.last_w[r] = op
            self.readers[r] = {}
        self.ops[eng].append(op)
        return op


def build_program():
    nc = bass.Bass("TRN2", target_bir_lowering=False)

    def din(name, shape):
        return nc.dram_tensor(name, list(shape), F32, kind="ExternalInput").ap()

    def dout(name, shape):
        return nc.dram_tensor(name, list(shape), F32, kind="ExternalOutput").ap()

    xa_d = din("xa", [P, NCH, 640])
    xb_d = din("xb", [P, NCH, 576])
    hmask_d = din("hmask", [P, 1])
    pcorr_d = din("pcorr", [P, 4, 16])
    stb_d = din("st_b", [P, 2, 2, 8, 2])
    stc_d = din("st_c", [P, 2, 2, 8, 30])
    std_d = din("st_d", [P, 2, 2, 8, 15])
    gall_d = din("g_all", [P, 5, NCH])
    cconv_d = din("cconv", [P, 2, 8, 31])
    bconv_d = din("bconv", [P, 2, 8, 3])
    cvec_d = din("cvec", [P, 2, 4, 8])
    wsT_d = din("wsT", [P, 2, 8, 128])
    triu_d = din("triu", [P, 128])
    ab_d = din("ab", [P, 2, 8, 128])
    dwT_d = din("dwT", [P, 2, 4, 2, 256])
    cmat_d = din("cmat", [P, 2, 128])
    w_in_d = din("w_in", [2, 4096, 8192])
    w_out_d = din("w_out", [2, 4096, 4096])
    w_ff1_d = din("w_ff1", [2, 4096, 16384])
    w_ff2_d = din("w_ff2", [2, 16384, 4096])

    ya_d = dout("ya", [P, NCH, 512])
    yb_d = dout("yb", [P, NCH, 576])
    oc_d = dout("oc", [P, 2, 3, 8, 30])
    ob_d = dout("ob", [P, 2, 3, 8, 2])
    od_d = dout("od", [P, 2, 3, 8, 15])
    oav_d = dout("oav", [32, 2, 2, 1024])

    R = Rec()

    import contextlib
    es = contextlib.ExitStack()

    def sb(name, shape, dt):
        return es.enter_context(nc.sbuf_tensor(name, list(shape), dt))

    with es:
        x = sb("x", [P, NCH, 576], F32)
        h = sb("h", [P, NCH, 640], BF16)
        mix = sb("mix", [P, 16, 576], BF16)
        wring = sb("wring", [P, 3, 4096], BF16)
        tmp = sb("tmp", [P, NTMP, LPAD], F32)
        vtok = sb("vtok", [P, 5, 256], BF16)
        vtoks = sb("vtoks", [32, 2, 256], BF16)
        avs = sb("avs", [32, 2, 256], F32)
        dbf = sb("dbf", [P, 2, LPAD], BF16)
        tlc = sb("tlc", [P, 2, 8, 30], F32)
        tlb = sb("tlb", [P, 2, 8, 2], F32)
        tld = sb("tld", [P, 2, 8, 15], F32)
        stc = sb("stc", [P, 2, 8, 30], F32)
        stb = sb("stb", [P, 2, 8, 2], F32)
        std = sb("std", [P, 2, 8, 15], F32)
        gall = sb("gall", [P, 5, NCH], F32)
        cconv = sb("cconvs", [P, 2, 8, 31], F32)
        bconv = sb("bconvs", [P, 2, 8, 3], F32)
        cvec = sb("cvecs", [P, 2, 4, 8], F32)
        cmat = sb("cmats", [P, 2, 128], F32)
        onesb = sb("onesb", [P, 128], BF16)
        hmask = sb("hmasks", [P, 1], F32)
        pcorr = sb("pcorrs", [P, 4, 16], F32)
        triu = sb("trius", [P, 128], BF16)
        wmT = sb("wmT", [P, 8, 128], BF16)
        abT = sb("abT", [P, 8, 128], F32)
        dwb = sb("dwb", [P, 4, 2, 256], BF16)
        rstd = sb("rstd", [P, 640], F32)
        zeros = sb("zeros", [P, 32], F32)
        dg = sb("dg", [P, 2, 6, 128], BF16)
        glub = sb("glub", [P, 2, LPAD], BF16)
        cbc = sb("cbc", [P, 8], F32)
        ps = es.enter_context(nc.psum_tensor("ps", [P, 8, 512], F32))

        def act(fn, reads, writes):
            return R.add("act", fn, reads, writes)

        def dve(fn, reads, writes):
            return R.add("dve", fn, reads, writes)

        def pe(fn, reads, writes):
            return R.add("pe", fn, reads, writes)

        def dma(q, out, in_, reads, writes, chan=None):
            return R.add(q, lambda e, o=out, i=in_: e.dma_start(out=o, in_=i), reads, writes, dma=True, chan=chan)

        def ACT(out, in_, func, reads, writes, bias=None, scale=None):
            kw = {}
            if bias is not None:
                kw["bias"] = bias
            if scale is not None:
                kw["scale"] = scale
            return act(lambda e, o=out, i=in_, f=func, kw=kw: e.activation(out=o, in_=i, func=f, **kw), reads, writes)

        def TT(out, in0, in1, op, reads, writes):
            return dve(lambda e, o=out, a=in0, b=in1, op=op: e.tensor_tensor(out=o, in0=a, in1=b, op=op), reads, writes)

        def STT(out, in0, scalar, in1, op0, op1, reads, writes):
            return dve(lambda e, o=out, a=in0, s=scalar, b=in1, o0=op0, o1=op1:
                       e.scalar_tensor_tensor(out=o, in0=a, scalar=s, in1=b, op0=o0, op1=o1), reads, writes)

        def TS(out, in0, s1, s2, op0, op1, reads, writes):
            if op1 is None:
                return dve(lambda e, o=out, a=in0, s1=s1, o0=op0: e.tensor_scalar(out=o, in0=a, scalar1=s1, scalar2=None, op0=o0), reads, writes)
            return dve(lambda e, o=out, a=in0, s1=s1, s2=s2, o0=op0, o1=op1:
                       e.tensor_scalar(out=o, in0=a, scalar1=s1, scalar2=s2, op0=o0, op1=o1), reads, writes)

        def ACOPY(out, in_, reads, writes):
            return act(lambda e, o=out, i=in_: e.copy(out=o, in_=i), reads, writes)

        def RECIP(ap, reads, writes):
            return dve(lambda e, a=ap: e.reciprocal(out=a, in_=a), reads, writes)

        def MM(out, lhsT, rhs, start, stop, reads, writes):
            return pe(lambda e, o=out, l=lhsT, r=rhs, s=start, t=stop: e.matmul(o, l, r, start=s, stop=t), reads, writes)

        dma("sp", gall[:], gall_d, [], [("c", "gall")])
        dma("sp", cconv[:], cconv_d, [], [("c", "cconv")])
        dma("sp", bconv[:], bconv_d, [], [("c", "bconv")])
        dma("sp", cvec[:], cvec_d, [], [("c", "cvec")])
        dma("sp", cmat[:], cmat_d, [], [("c", "cmat")])
        dma("sp", hmask[:], hmask_d, [], [("c", "hmask")])
        dma("sp", pcorr[:], pcorr_d, [], [("c", "pcorr")])
        dma("pool", triu[:], triu_d, [], [("c", "triu")])
        dve(lambda e: e.memset(onesb[:], 1.0), [], [("c", "ones")])
        dve(lambda e: e.memset(zeros[:], 0.0), [], [("c", "zeros")])
        dve(lambda e: e.memset(tmp[:], 0.0), [], [("tmp", i) for i in range(NTMP)])
        dve(lambda e: e.memset(dbf[:], 0.0), [("dbf", 0), ("dbf", 1)], [("dbf", 0), ("dbf", 1)])

        wstate = {"slot": 0}

        def wload(Wv, kc0, nk, c0, ncol):
            s = wstate["slot"]
            wstate["slot"] = (s + 1) % 3
            view = wring[:, s, 0:nk * ncol].rearrange("p (k c) -> p k c", k=nk)
            dma("pool", view, Wv[:, kc0:kc0 + nk, c0:c0 + ncol], [], [("w", s)], chan=("w", s))
            return s, view

        pending = []
        epoch = {"e": 0}

        def push_job(fn):
            pending.append((epoch["e"], fn))

        def run_pending(n):
            for _ in range(n):
                if pending and pending[0][0] < epoch["e"]:
                    pending.pop(0)[1]()
            epoch["e"] += 1

        def flush_pending():
            while pending:
                pending.pop(0)[1]()

        def ntiles(t0, t1):
            n = (t1 - t0) // 2
            return [(t0, t0 + n), (t0 + n, t1)]

        def stream_fm(Wv, kchunks, col0, ncols, KB, CB, rhs_fn, rhs_reg, tiles, evac, bank0=0, korder=None):
            nkb = len(kchunks) // KB
            nct = CB // 128
            for cb in range(ncols // CB):
                c0 = col0 + cb * CB
                for kb in range(nkb):
                    s, view = wload(Wv, kchunks[kb * KB], KB, c0, CB)
                    kks = list(range(KB)) if korder is None else korder
                    for ct in range(nct):
                        for kpos, kk in enumerate(kks):
                            kidx = kb * KB + kk
                            for ti, (a, b) in enumerate(tiles):
                                bank = bank0 + ct * len(tiles) + ti
                                MM(ps[:, bank, 0:b - a], view[:, kk, ct * 128:(ct + 1) * 128], rhs_fn(kidx, a, b),
                                   kb == 0 and kpos == 0, kb == nkb - 1 and kpos == KB - 1,
                                   [("w", s), rhs_reg(kidx)], [("ps", bank)])
                    if kb < nkb - 1:
                        run_pending(2)
                for ct in range(nct):
                    banks = [bank0 + ct * len(tiles) + ti for ti in range(len(tiles))]
                    evac((c0 - col0) // 128 + ct, banks)
                run_pending(2)

        def norm_sq(ch, xap, xregion, ntok, slot, defer, first_of_buf):
            tl = ntiles(0, ntok) if ntok > 512 else [(0, ntok)]
            bi = slot // 2
            sqv = tmp[:, bi, :].bitcast(BF16)
            o_ = (slot % 2) * LPAD
            ACT(sqv[:, o_:o_ + ntok], xap, AF.Square, [xregion, ("tmp", bi)],
                [("sqs", slot)] + ([("tmp", bi)] if first_of_buf else []))

            def job(ch=ch):
                for ti, (a, b) in enumerate(tl):
                    MM(ps[:, 4 + ti, 0:b - a], onesb[:, :], sqv[:, o_ + a:o_ + b], ch == 0, ch == NCH - 1,
                       [("sqs", slot), ("tmp", bi), ("c", "ones")], [("ps", 4 + ti)])
            if defer:
                push_job(job)
            else:
                job()

        def norm_fin(xsrc, xreg, ntok, gi, hdst, hreg):
            flush_pending()
            tl = ntiles(0, ntok) if ntok > 512 else [(0, ntok)]
            for ti, (a, b) in enumerate(tl):
                ACT(rstd[:, a:b], ps[:, 4 + ti, 0:b - a], AF.Sqrt, [("ps", 4 + ti)], [("rstd", ti)],
                    bias=epsb[:, 0:1], scale=1.0 / 4096.0)
                RECIP(rstd[:, a:b], [("rstd", ti)], [("rstd", ti)])
            for ch in range(NCH):
                STT(hdst(ch), xsrc(ch), gall[:, gi, ch:ch + 1], rstd[:, 0:ntok], ALU.mult, ALU.mult,
                    [xreg(ch), ("c", "gall"), ("rstd", 0), ("rstd", 1)], [hreg(ch)])

        def rmsnorm(xsrc, xreg, ntok, gi, hdst, hreg):
            for ch in range(NCH):
                norm_sq(ch, xsrc(ch), xreg(ch), ntok, 10 + ch % 2, False, ch == 0)
            norm_fin(xsrc, xreg, ntok, gi, hdst, hreg)

        epsb = sb("epsb", [P, 1], F32)
        dve(lambda e: e.memset(epsb[:], EPS), [], [("c", "eps")])

        def run_pass(pi):
            if pi == 0:
                TH, XOFF = 640, 96
                segs_all = [(0, 640, 30)]
                nchunks = 5
            else:
                TH, XOFF = 576, 0
                segs_all = [(0, 512, 30), (512, 544, 572), (544, 576, 634)]
                nchunks = 4
            TX = TH - XOFF
            L = 670 if pi == 0 else 666
            Lc = L - 30

            if pi == 0:
                xfar = tmp[:, 0:5, :].rearrange("p a b -> p (a b)")[:, 0:3072].rearrange("p (c t) -> p c t", c=NCH)
                dma("sp", xfar, xa_d[:, :, 0:96], [], [("tmp", i) for i in range(5)])
                for q in range(4):
                    dma("sp", x[:, q * 8:(q + 1) * 8, 0:544], xa_d[:, q * 8:(q + 1) * 8, 96:640],
                        [], [("x", c) for c in range(q * 8, q * 8 + 8)])
            else:
                for q in range(4):
                    dma("sp", x[:, q * 8:(q + 1) * 8, 0:576], xb_d[:, q * 8:(q + 1) * 8, :],
                        [], [("x", c) for c in range(q * 8, q * 8 + 8)])

            for l in range(DEPTH):
                t0 = 0 if (pi == 1 or l == 0) else 96
                segs = [(max(ts, t0), te, pp + max(ts, t0) - ts) for (ts, te, pp) in segs_all]
                wtiles = ntiles(t0, TH)

                dma("pool", wmT[:], wsT_d[:, l], [], [("c", "wmT")])
                dve(lambda e: [e.tensor_tensor(out=wmT[:, hh, :], in0=wmT[:, hh, :], in1=triu[:, :], op=ALU.mult)
                               for hh in range(8)][-1], [("c", "wmT"), ("c", "triu")], [("c", "wmT")])
                dma("sp", abT[:], ab_d[:, l], [], [("c", "abT")])
                if pi == 1:
                    dma("sp", stc[:], stc_d[:, l], [], [("c", "stc")])
                    dma("sp", stb[:], stb_d[:, l], [], [("c", "stb")])
                    dma("sp", std[:], std_d[:, l], [], [("c", "std")])
                dma("pool", dwb[:], dwT_d[:, l], [], [("c", "dwb")])

                MM(ps[:, 7, 0:8], cmat[:, 0, :], cvec[:, l, 0, :], True, True, [("c", "cmat"), ("c", "cvec")], [("ps", 7)])
                ACT(cbc[:, :], ps[:, 7, 0:8], AF.Copy, [("ps", 7)], [("c", "cbc")])

                if pi == 0 and l == 0:
                    rmsnorm(lambda ch: xfar[:, ch, :], lambda ch: ("tmp", (ch * 96) // LPAD), 96, 0,
                            lambda ch: h[:, ch, 0:96], lambda ch: ("h", ch))
                if l == 0:
                    rmsnorm(lambda ch: x[:, ch, 0:TX], lambda ch: ("x", ch), TX, 2 * l,
                            lambda ch: h[:, ch, XOFF:TH], lambda ch: ("h", ch))
                else:
                    norm_fin(lambda ch: x[:, ch, 0:TX], lambda ch: ("x", ch), TX, 2 * l,
                             lambda ch: h[:, ch, XOFF:TH], lambda ch: ("h", ch))

                Win = w_in_d[l].rearrange("(kc p) n -> p kc n", p=P)
                Wout = w_out_d[l].rearrange("(kc p) n -> p kc n", p=P)
                W1 = w_ff1_d[l].rearrange("(kc p) n -> p kc n", p=P)
                W2 = w_ff2_d[l].rearrange("(kc p) n -> p kc n", p=P)
                allk = list(range(NCH))
                hrhs = lambda k, a, b: h[:, k, a:b]
                hreg = lambda k: ("h", k)

                def seg_pieces(a, b):
                    out = []
                    for (ts, te, pp) in segs:
                        lo, hi = max(a, ts), min(b, te)
                        if lo < hi:
                            out.append((lo, hi, pp + lo - ts))
                    return out

                def evac_to_pad(banks, dst_i, func, extra_reads=()):
                    for bank, (a, b) in zip(banks, wtiles):
                        for (lo, hi, pl) in seg_pieces(a, b):
                            ACT(tmp[:, dst_i, pl:pl + hi - lo], ps[:, bank, lo - a:hi - a], func,
                                [("ps", bank)] + list(extra_reads), [("tmp", dst_i)])

                def fill_hist(dst_i, nh, tl_t, st_t, ch):
                    if pi == 0:
                        ACOPY(tmp[:, dst_i, 30 - nh:30], zeros[:, 0:nh], [("c", "zeros")], [("tmp", dst_i)])
                    else:
                        ACOPY(tmp[:, dst_i, 30 - nh:30], tl_t[:, l, ch, :], [("tl", nh, l, ch)], [("tmp", dst_i)])
                        for sq_ in range(2):
                            pp = segs_all[1 + sq_][2]
                            ACOPY(tmp[:, dst_i, pp - nh:pp], st_t[:, sq_, ch, :],
                                  [("c", "stc"), ("c", "stb"), ("c", "std")], [("tmp", dst_i)])

                def save_tails(src_i, nh, tl_t, out_d, ch):
                    if pi == 0:
                        ACOPY(tl_t[:, l, ch, :], tmp[:, src_i, 670 - nh:670], [("tmp", src_i)], [("tl", nh, l, ch)])
                    else:
                        for si, (ts, te, pp) in enumerate(segs_all):
                            pe_ = pp + te - ts
                            dma("sp", out_d[:, l, si, ch, :], tmp[:, src_i, pe_ - nh:pe_], [("tmp", src_i)], [])

                def pad_to_mix(src_ap_fn, mt, emit):
                    for (ts, te, pp) in segs_all:
                        lo = max(ts, XOFF)
                        if lo >= te:
                            continue
                        j0 = pp + (lo - ts) - 30
                        emit(mix[:, mt, lo - XOFF:te - XOFF], src_ap_fn(j0, j0 + te - lo))

                for i in range(4):
                    def evac_cg(ct, banks, i=i):
                        p_ = ct % 2
                        evac_to_pad(banks, p_, AF.Sigmoid)
                    stream_fm(Win, allk, 6144 + 256 * i, 256, 16, 256, hrhs, hreg, wtiles, evac_cg)

                    def evac_ca(ct, banks, i=i):
                        p_ = ct % 2
                        ch = 2 * i + p_
                        S, G, Cn, Sq = p_, 2, 3 + p_, 5
                        if p_ == 0:
                            flush_pending()
                        evac_to_pad(banks, G, AF.Copy)
                        TT(tmp[:, G, 30:L], tmp[:, G, 30:L], tmp[:, S, 30:L], ALU.mult, [("tmp", G), ("tmp", S)], [("tmp", G)])
                        fill_hist(G, 30, tlc, stc, ch)
                        save_tails(G, 30, tlc, oc_d, ch)
                        dve(lambda e, o=glub[:, p_, 0:L], a=tmp[:, G, 0:L]: e.tensor_copy(out=o, in_=a), [("tmp", G)], [("glub", p_)])
                        ct2 = ntiles(0, Lc)

                        def build_dg(tg, ch=ch):
                            for j, k in enumerate(range(6 * tg, min(6 * tg + 6, 31))):
                                TS(dg[:, tg % 2, j, :], cmat[:, 0, :], cconv[:, l, ch, k:k + 1], None, ALU.mult, None,
                                   [("c", "cmat"), ("c", "cconv")], [("dg", tg % 2)])

                        def job_diag():
                            build_dg(0)
                            build_dg(1)

                        def job_conv(ch=ch, p_=p_):
                            for tg in range(6):
                                taps = list(range(6 * tg, min(6 * tg + 6, 31)))
                                if tg >= 2:
                                    build_dg(tg)
                                for j, k in enumerate(taps):
                                    for ti, (a, b) in enumerate(ct2):
                                        MM(ps[:, 4 + ti, 0:b - a], dg[:, tg % 2, j, :], glub[:, p_, k + a:k + b], k == 0, k == 30,
                                           [("dg", tg % 2), ("glub", p_)], [("ps", 4 + ti)])
                            for ti, (a, b) in enumerate(ct2):
                                ACT(tmp[:, Cn, a:b], ps[:, 4 + ti, 0:b - a], AF.Identity, [("ps", 4 + ti), ("c", "cbc")], [("tmp", Cn)],
                                    bias=cbc[:, ch:ch + 1])
                                ACT(tmp[:, Sq, a:b], ps[:, 4 + ti, 0:b - a], AF.Square, [("ps", 4 + ti), ("c", "cbc")], [("tmp", Sq)],
                                    bias=cbc[:, ch:ch + 1])

                        def job_var(ch=ch, p_=p_):
                            for ti, (a, b) in enumerate(ct2):
                                MM(ps[:, 6 + ti, 0:b - a], cmat[:, 1, :], tmp[:, Sq, a:b], True, True,
                                   [("c", "cmat"), ("tmp", Sq)], [("ps", 6 + ti)])
                            for ti, (a, b) in enumerate(ct2):
                                ACT(tmp[:, Sq, a:b], ps[:, 6 + ti, 0:b - a], AF.Sqrt, [("ps", 6 + ti)], [("tmp", Sq)],
                                    bias=epsb[:, 0:1], scale=1.0)
                            RECIP(tmp[:, Sq, 0:Lc], [("tmp", Sq)], [("tmp", Sq)])
                            TT(tmp[:, Cn, 0:Lc], tmp[:, Cn, 0:Lc], tmp[:, Sq, 0:Lc], ALU.mult, [("tmp", Cn), ("tmp", Sq)], [("tmp", Cn)])
                            pad_to_mix(lambda j0, j1: tmp[:, Cn, j0:j1], ch,
                                       lambda mo, so: ACT(mo, so, AF.Silu, [("tmp", Cn), ("c", "cvec")], [("mix", ch)],
                                                          bias=cvec[:, l, 2, ch:ch + 1], scale=cvec[:, l, 1, ch:ch + 1]))
                        push_job(job_diag)
                        push_job(job_conv)
                        push_job(job_var)
                    stream_fm(Win, allk, 5120 + 256 * i, 256, 16, 256, hrhs, hreg, wtiles, evac_ca)

                for g in range(4):
                    def evac_dp(ct, banks, g=g):
                        p_ = ct % 2
                        ch = 2 * g + p_
                        D0, A1, A2 = p_, 2 + 2 * p_, 3 + 2 * p_
                        if p_ == 0:
                            flush_pending()
                        evac_to_pad(banks, D0, AF.Copy)
                        fill_hist(D0, 15, tld, std, ch)
                        save_tails(D0, 15, tld, od_d, ch)
                        w = POOLW[g]
                        src, vs, step = D0, 15, 1
                        pp_ = [A1, A2]
                        k = 0
                        while step < w:
                            dst = pp_[k % 2]
                            nv = vs + step
                            TT(tmp[:, dst, nv:L], tmp[:, src, nv:L], tmp[:, src, nv - step:L - step], ALU.add,
                               [("tmp", src)], [("tmp", dst)])
                            src, vs, step, k = dst, nv, step * 2, k + 1
                        if pi == 0:
                            TT(tmp[:, src, 158:174], tmp[:, src, 158:174], pcorr[:, g, :], ALU.mult,
                               [("tmp", src), ("c", "pcorr")], [("tmp", src)])
                        STT(dbf[:, p_, 30:L], tmp[:, src, 30:L], 1.0 / w, tmp[:, D0, 30:L], ALU.mult, ALU.subtract,
                            [("tmp", src), ("tmp", D0)], [("dbf", p_)])
                        def job_dmap(g=g):
                            dtl = ntiles(30, L)
                            for dt_ in range(2):
                                for kc in range(2):
                                    for ti, (a, b) in enumerate(dtl):
                                        bank = 4 + dt_ * 2 + ti
                                        MM(ps[:, bank, 0:b - a], dwb[:, g, kc, dt_ * 128:(dt_ + 1) * 128], dbf[:, kc, a:b],
                                           kc == 0, kc == 1, [("c", "dwb"), ("dbf", kc)], [("ps", bank)])
                            for dt_ in range(2):
                                chd = 2 * g + dt_
                                for ti, (a, b) in enumerate(dtl):
                                    bank = 4 + dt_ * 2 + ti
                                    for (ts, te, pp) in segs_all:
                                        lo_t = max(ts, XOFF)
                                        if lo_t >= te:
                                            continue
                                        plo, phi = pp + lo_t - ts, pp + te - ts
                                        qlo, qhi = max(plo, a), min(phi, b)
                                        if qlo >= qhi:
                                            continue
                                        tlo = lo_t + (qlo - plo)
                                        ACT(mix[:, 8 + chd, tlo - XOFF:tlo - XOFF + qhi - qlo], ps[:, bank, qlo - a:qhi - a],
                                            AF.Copy, [("ps", bank), ("c", "cvec")], [("mix", 8 + chd)],
                                            scale=cvec[:, l, 3, chd:chd + 1])
                        if p_ == 1:
                            push_job(job_dmap)
                    stream_fm(Win, allk, 7168 + 256 * g, 256, 16, 256, hrhs, hreg, wtiles, evac_dp)
                def b_first(i):
                    def evac_bh(ct, banks):
                        evac_to_pad(banks, 2 * (ct % 2), AF.Copy)
                    stream_fm(Win, allk, 2048 + 256 * i, 256, 16, 256, hrhs, hreg, wtiles, evac_bh)

                    def evac_bc(ct, banks, i=i):
                        p_ = ct % 2
                        ch = 2 * i + p_
                        H, Cb = 2 * p_, 2 * p_ + 1
                        evac_to_pad(banks, Cb, AF.Copy)
                        TT(tmp[:, Cb, 30:L], tmp[:, Cb, 30:L], tmp[:, H, 30:L], ALU.mult, [("tmp", Cb), ("tmp", H)], [("tmp", Cb)])
                        fill_hist(Cb, 2, tlb, stb, ch)
                        save_tails(Cb, 2, tlb, ob_d, ch)
                        TS(tmp[:, H, 0:Lc], tmp[:, Cb, 28:28 + Lc], bconv[:, l, ch, 0:1], None, ALU.mult, None,
                           [("tmp", Cb), ("c", "bconv")], [("tmp", H)])
                        for k in (1, 2):
                            STT(tmp[:, H, 0:Lc], tmp[:, Cb, 28 + k:28 + k + Lc], bconv[:, l, ch, k:k + 1], tmp[:, H, 0:Lc],
                                ALU.mult, ALU.add, [("tmp", Cb), ("tmp", H), ("c", "bconv")], [("tmp", H)])
                    stream_fm(Win, allk, 4096 + 256 * i, 256, 16, 256, hrhs, hreg, wtiles, evac_bc)

                def b_second(i):
                    def evac_bb(ct, banks, i=i):
                        p_ = ct % 2
                        ch = 2 * i + p_
                        H, Cb = 2 * p_, 2 * p_ + 1
                        evac_to_pad(banks, Cb, AF.Copy)
                        for (ts, te, pp) in segs_all:
                            lo = max(ts, XOFF)
                            if lo >= te:
                                continue
                            p0 = pp + lo - ts
                            n = te - lo
                            TT(mix[:, 8 + ch, lo - XOFF:te - XOFF], tmp[:, Cb, p0:p0 + n], tmp[:, H, p0 - 30:p0 - 30 + n], ALU.mult,
                               [("tmp", Cb), ("tmp", H)], [("mix", 8 + ch)])
                    stream_fm(Win, allk, 3072 + 256 * i, 256, 16, 256, hrhs, hreg, wtiles, evac_bb)

                b_first(0)
                flush_pending()
                xtiles = ntiles(32 if (pi == 0 and l == 1) else 0, TX)

                def evac_x(ct, banks):
                    for bank, (a, b) in zip(banks, xtiles):
                        TT(x[:, ct, a:b], ps[:, bank, 0:b - a], x[:, ct, a:b], ALU.add, [("ps", bank), ("x", ct)], [("x", ct)])

                stream_fm(Wout, list(range(16, 32)), 0, 4096, 16, 256, lambda k, a, b: mix[:, k, a:b],
                          lambda k: ("mix", k), xtiles, evac_x)

                b_second(0)
                for i in range(1, 4):
                    b_first(i)
                    b_second(i)

                for i in range(4):
                    c0 = 1024 + 256 * i
                    s0, v0 = wload(Win, 0, 16, c0, 256)
                    s1, v1 = wload(Win, 16, 16, c0, 256)
                    views = [(s0, v0), (s1, v1)]
                    jobs = [(tc * 128, tc * 128 + 128, tc, None) for tc in range(nchunks)]
                    if pi == 1:
                        jobs += [(512, 544, None, 0), (544, 576, None, 1)]
                    avbanks = [0, 1, 2, 3, 6, 7]
                    for khalf in range(2):
                        s_, v_ = views[khalf]
                        for ji, (a, b, tc, sq_) in enumerate(jobs):
                            bank = avbanks[ji]
                            m = b - a
                            for kk in range(16):
                                kc = khalf * 16 + kk
                                MM(ps[0:m, bank, 0:256], h[:, kc, a:b], v_[:, kk, :], kc == 0, kc == NCH - 1,
                                   [("w", s_), ("h", kc)], [("ps", bank)])
                        if khalf == 0:
                            flush_pending()
                    for ji, (a, b, tc, sq_) in enumerate(jobs):
                        bank = avbanks[ji]
                        if tc is not None:
                            ACT(vtok[:, tc, :], ps[:, bank, 0:256], AF.Gelu, [("ps", bank)], [("vtok", tc)])
                        else:
                            ACT(avs[0:32, sq_, :], ps[0:32, bank, 0:256], AF.Gelu, [("ps", bank)], [("avs", sq_)])
                            ACOPY(vtoks[0:32, sq_, :], avs[0:32, sq_, :], [("avs", sq_)], [("vtoks", sq_)])
                            dma("sp", oav_d[0:32, l, sq_, 256 * i:256 * i + 256], avs[0:32, sq_, :], [("avs", sq_)], [])

                    def evac_au(ct, banks, i=i):
                        hd = 2 * i + ct
                        for bank, (a, b) in zip(banks, wtiles):
                            lo = max(a, XOFF)
                            if lo < b:
                                ACT(mix[:, hd, lo - XOFF:b - XOFF], ps[:, bank, lo - a:b - a], AF.Gelu, [("ps", bank)], [("mix", hd)])
                    stream_fm(Win, allk, 256 * i, 256, 16, 256, hrhs, hreg, wtiles, evac_au)

                    def job_gate(i=i):
                        for hh in range(2):
                            hd = 2 * i + hh
                            gj = []
                            for tc in range(nchunks):
                                bank, cc = (4, tc * 128) if tc < 4 else (5, 0)
                                MM(ps[:, bank, cc:cc + 128], vtok[:, tc, hh * 128:(hh + 1) * 128], wmT[:, hd, :], True, True,
                                   [("vtok", tc), ("c", "wmT")], [("ps", bank)])
                                gj.append((bank, cc, 128, tc * 128, 0))
                            if pi == 1:
                                for sq_ in range(2):
                                    MM(ps[:, 5, 32 * sq_:32 * sq_ + 32], vtoks[0:32, sq_, hh * 128:(hh + 1) * 128], wmT[0:32, hd, 0:32],
                                       True, True, [("vtoks", sq_), ("c", "wmT")], [("ps", 5)])
                                    gj.append((5, 32 * sq_, 32, 512 + 32 * sq_, 0))
                            for (bank, cc, n, tlo, bl) in gj:
                                lo = max(tlo, XOFF)
                                if lo >= tlo + n:
                                    continue
                                sk = lo - tlo
                                n2 = n - sk
                                gt = tmp[:, 4 + hh, 0:n2]
                                TT(gt, ps[:, bank, cc + sk:cc + n], abT[:, hd, bl + sk:bl + n], ALU.add,
                                   [("ps", bank), ("c", "abT")], [("tmp", 4 + hh)])
                                TT(mix[:, hd, lo - XOFF:lo - XOFF + n2], gt, mix[:, hd, lo - XOFF:lo - XOFF + n2], ALU.mult,
                                   [("tmp", 4 + hh), ("mix", hd)], [("mix", hd)])
                    push_job(job_gate)
                flush_pending()

                def evac_x_stats(ct, banks, mask=False):
                    evac_x(ct, banks)
                    if mask:
                        TS(x[:, ct, 0:32], x[:, ct, 0:32], hmask[:, 0:1], None, ALU.mult, None,
                           [("x", ct), ("c", "hmask")], [("x", ct)])
                    norm_sq(ct, x[:, ct, 0:TX], ("x", ct), TX, ct % 12, True, ct < 12 and ct % 2 == 0)

                stream_fm(Wout, list(range(0, 16)), 0, 4096, 16, 256, lambda k, a, b: mix[:, k, a:b],
                          lambda k: ("mix", k), xtiles, evac_x_stats, korder=list(range(8, 16)) + list(range(0, 8)))

                norm_fin(lambda ch: x[:, ch, 0:TX], lambda ch: ("x", ch), TX, 2 * l + 1,
                         lambda ch: h[:, ch, XOFF:TH], lambda ch: ("h", ch))
                hx = lambda k, a, b: h[:, k, XOFF + a:XOFF + b]
                rot = {"i": 0}
                for grp in range(8):
                    def evac_f1(ct, banks, grp=grp):
                        for bank, (a, b) in zip(banks, xtiles):
                            ti_ = rot["i"] % 4
                            rot["i"] += 1
                            ACT(tmp[:, ti_, 0:b - a], ps[:, bank, 0:b - a], AF.Relu, [("ps", bank)], [("tmp", ti_)])
                            TT(mix[:, ct, a:b], tmp[:, ti_, 0:b - a], tmp[:, ti_, 0:b - a], ALU.mult, [("tmp", ti_)], [("mix", ct)])
                    stream_fm(W1, allk, grp * 2048, 2048, 8, 512, hx, hreg, xtiles, evac_f1)
                    if grp < 7:
                        stream_fm(W2, list(range(grp * 16, grp * 16 + 16)), 0, 4096, 8, 512, lambda k, a, b: mix[:, k, a:b],
                                  lambda k: ("mix", k), xtiles, evac_x)
                    else:
                        stream_fm(W2, list(range(grp * 16, grp * 16 + 16)), 0, 4096, 16, 256, lambda k, a, b: mix[:, k, a:b],
                                  lambda k: ("mix", k), xtiles,
                                  lambda ct, banks: evac_x_stats(ct, banks, mask=(pi == 0 and l == 0)))

            norm_fin(lambda ch: x[:, ch, 0:TX], lambda ch: ("x", ch), TX, 4,
                     lambda ch: x[:, ch, 0:TX], lambda ch: ("x", ch))
            for q in range(4):
                regs = [("x", c) for c in range(q * 8, q * 8 + 8)]
                if pi == 0:
                    dma("sp", ya_d[:, q * 8:(q + 1) * 8, :], x[:, q * 8:(q + 1) * 8, 32:544], regs, [])
                else:
                    dma("sp", yb_d[:, q * 8:(q + 1) * 8, :], x[:, q * 8:(q + 1) * 8, 0:576], regs, [])

        run_pass(0)
        run_pass(1)

        sem_names = set()
        for e in Rec.ENGS:
            cnt = 0
            for op in R.ops[e]:
                if op.is_dma:
                    sem_names.add(op.sem)
                elif op.flagged:
                    cnt += 1
                    op.val = cnt
                    op.sem = ("prog", e)
            sem_names.add(("prog", e))
        sem_names = sorted(sem_names, key=str)
        sems = {}
        for i, k in enumerate(sem_names):
            sems[k] = es.enter_context(nc.semaphore("s%d" % i))

        out_dmas = [op for e in Rec.ENGS for op in R.ops[e] if op.is_dma]

        def emit(engobj, ename):
            waited = {}
            for op in R.ops[ename]:
                need = {}
                for d in op.deps:
                    if d.val is None:
                        continue
                    if need.get(d.sem, 0) < d.val:
                        need[d.sem] = d.val
                for k, v in need.items():
                    if waited.get(k, 0) < v:
                        engobj.wait_ge(sems[k], v)
                        waited[k] = v
                ins = op.fn(engobj)
                if op.is_dma:
                    ins.then_inc(sems[op.sem], 16)
                elif op.flagged:
                    ins.then_inc(sems[op.sem], 1)
            if ename == "sp":
                fin = {}
                for op in out_dmas:
                    if fin.get(op.sem, 0) < op.val:
                        fin[op.sem] = op.val
                for k, v in fin.items():
                    engobj.wait_ge(sems[k], v)

        with nc.Block() as block:
            @block.tensor
            def _(e):
                emit(e, "pe")

            @block.scalar
            def _(e):
                emit(e, "act")

            @block.vector
            def _(e):
                emit(e, "dve")

            @block.gpsimd
            def _(e):
                emit(e, "pool")

            @block.sync
            def _(e):
                emit(e, "sp")
    return nc


def _fm(a):
    T, C = a.shape
    return np.ascontiguousarray(a.reshape(T, C // 128, 128).transpose(2, 1, 0))


def _fm_inv(a):
    p, n, T = a.shape
    return np.ascontiguousarray(a.transpose(2, 1, 0).reshape(T, n * 128))


_NC_CACHE = {}


def kernel(x_prompt, x_sample, state_conv_b, state_conv_c, state_pool, g_mix, w_in, a_ws, a_b, b_conv,
           c_conv, c_conv_b, c_ln_g, c_ln_b, d_w, d_scale, w_out, g_ffn, w_ff1, w_ff2, g_final):
    f = lambda a: np.ascontiguousarray(np.asarray(a, dtype=np.float32))
    x_prompt, x_sample = f(x_prompt), f(x_sample)
    state_conv_b, state_conv_c, state_pool = f(state_conv_b), f(state_conv_c), f(state_pool)
    w_in, w_out, w_ff1, w_ff2 = f(w_in), f(w_out), f(w_ff1), f(w_ff2)
    g_mix, g_ffn, g_final = f(g_mix), f(g_ffn), f(g_final)
    a_ws, a_b, b_conv, c_conv = f(a_ws), f(a_b), f(b_conv), f(c_conv)
    c_conv_b, c_ln_g, c_ln_b, d_w, d_scale = f(c_conv_b), f(c_ln_g), f(c_ln_b), f(d_w), f(d_scale)

    def vec_fm(v, nchunk):
        return v.reshape(nchunk, 128).T

    g_all = np.ascontiguousarray(np.stack([vec_fm(g_mix[0], 32), vec_fm(g_ffn[0], 32), vec_fm(g_mix[1], 32),
                                           vec_fm(g_ffn[1], 32), vec_fm(g_final, 32)], axis=1))
    cconv = np.ascontiguousarray(c_conv.reshape(2, 31, 8, 128).transpose(3, 0, 2, 1))
    bconv = np.ascontiguousarray(b_conv.reshape(2, 3, 8, 128).transpose(3, 0, 2, 1))
    cvec = np.ascontiguousarray(np.stack([c_conv_b, c_ln_g, c_ln_b, d_scale], axis=1).reshape(2, 4, 8, 128).transpose(3, 0, 1, 2))
    wsT = np.ascontiguousarray(a_ws.transpose(3, 0, 1, 2))
    jj, ii = np.meshgrid(np.arange(128), np.arange(128), indexing="ij")
    triu = np.ascontiguousarray((jj <= ii).astype(np.float32))
    ab = np.ascontiguousarray(np.broadcast_to(a_b[None], (128, 2, 8, 128)))
    dwT = np.ascontiguousarray(d_w.reshape(2, 4, 2, 128, 256).transpose(3, 0, 1, 2, 4))
    cm = np.stack([np.eye(128, dtype=np.float32) - 1.0 / 128.0, np.full((128, 128), 1.0 / 128.0, np.float32)], axis=1)
    cmat = np.ascontiguousarray(cm.astype(np.float32))

    in_maps = []
    for c in range(8):
        b, k = c // 4, c % 4
        s = 1024 * k
        xa = np.zeros((640, 4096), np.float32)
        if k > 0:
            xa[:] = x_prompt[b, s - 128:s + 512]
        else:
            xa[128:] = x_prompt[b, 0:512]
        xb = np.concatenate([x_prompt[b, s + 512:s + 1024], x_sample[2 * c], x_sample[2 * c + 1]], axis=0)
        pc = np.ones((128, 4, 16), np.float32)
        if k == 0:
            for g, w in enumerate(POOLW):
                for t in range(16):
                    pc[:, g, t] = np.float32(w) / np.float32(min(t + 1, w))
        sq = [2 * c, 2 * c + 1]

        def st_fm(st, nr):
            a = st[:, sq]
            return np.ascontiguousarray(a.reshape(2, 2, nr, 8, 128).transpose(4, 0, 1, 3, 2))
        in_maps.append({
            "xa": _fm(xa), "xb": _fm(xb),
            "hmask": np.full((128, 1), 0.0 if k == 0 else 1.0, np.float32),
            "pcorr": pc,
            "st_b": st_fm(state_conv_b, 2), "st_c": st_fm(state_conv_c, 30), "st_d": st_fm(state_pool, 15),
            "g_all": g_all, "cconv": cconv, "bconv": bconv, "cvec": cvec, "wsT": wsT, "triu": triu, "ab": ab,
            "dwT": dwT, "cmat": cmat, "w_in": w_in, "w_out": w_out, "w_ff1": w_ff1, "w_ff2": w_ff2,
        })

    if "nc" not in _NC_CACHE:
        _NC_CACHE["nc"] = build_program()
    nc = _NC_CACHE["nc"]
    res = run_bass_kernel_spmd(nc, in_maps, core_ids=list(range(8)))
    outs = res.results

    y_prompt = np.zeros((2, 4096, 4096), np.float32)
    y_sample = np.zeros((16, 32, 4096), np.float32)
    ncb_p = np.zeros((2, 2, 2, 1024), np.float32)
    ncc_p = np.zeros((2, 2, 30, 1024), np.float32)
    npl_p = np.zeros((2, 2, 15, 1024), np.float32)
    ncb_s = np.zeros((2, 16, 2, 1024), np.float32)
    ncc_s = np.zeros((2, 16, 30, 1024), np.float32)
    npl_s = np.zeros((2, 16, 15, 1024), np.float32)
    nav_s = np.zeros((2, 16, 32, 1024), np.float32)

    def tail_inv(a):
        return a.transpose(2, 1, 0).reshape(a.shape[2], 1024)

    for c in range(8):
        b, k = c // 4, c % 4
        s = 1024 * k
        o = outs[c]
        y_prompt[b, s:s + 512] = _fm_inv(o["ya"])
        yb = _fm_inv(o["yb"])
        y_prompt[b, s + 512:s + 1024] = yb[0:512]
        y_sample[2 * c] = yb[512:544]
        y_sample[2 * c + 1] = yb[544:576]
        for l in range(2):
            for si in range(3):
                tb, tc_, td = tail_inv(o["ob"][:, l, si]), tail_inv(o["oc"][:, l, si]), tail_inv(o["od"][:, l, si])
                if si == 0:
                    if k == 3:
                        ncb_p[l, b], ncc_p[l, b], npl_p[l, b] = tb, tc_, td
                else:
                    q = 2 * c + si - 1
                    ncb_s[l, q], ncc_s[l, q], npl_s[l, q] = tb, tc_, td
            for si in range(2):
                nav_s[l, 2 * c + si] = o["oav"][:, l, si, :]
    return (y_prompt, y_sample, ncb_p, ncc_p, npl_p, ncb_s, ncc_s, npl_s, nav_s)
```

```python
import numpy as np
import concourse.bass as bass
import concourse.mybir as mybir
from concourse.bass_utils import run_bass_kernel_spmd

F32 = mybir.dt.float32
BF16 = mybir.dt.bfloat16
AF = mybir.ActivationFunctionType
ALU = mybir.AluOpType

P = 128
NCH = 32
DEPTH = 2
EPS = 1e-6
NTMP = 6
LPAD = 672
POOLW = (2, 4, 8, 16)


class Op:
    __slots__ = ("eng", "fn", "deps", "flagged", "val", "sem", "is_dma", "key")

    def __init__(self, eng, fn, is_dma=False):
        self.eng = eng
        self.fn = fn
        self.deps = ()
        self.flagged = False
        self.val = None
        self.sem = None
        self.is_dma = is_dma
        self.key = eng


class Rec:
    ENGS = ("pe", "act", "dve", "pool", "sp")

    def __init__(self):
        self.ops = {e: [] for e in self.ENGS}
        self.last_w = {}
        self.readers = {}
        self.nchan = 8
        self.chan_use = [0] * self.nchan
        self.chan_last = [None] * self.nchan
        self.chan_rr = 0
        self.named_chan = {}
        self.ndma = 0

    def add(self, eng, fn, reads=(), writes=(), dma=False, chan=None):
        op = Op(eng, fn, dma)
        deps = set()
        for r in reads:
            w = self.last_w.get(r)
            if w is not None:
                deps.add(w)
        for r in writes:
            w = self.last_w.get(r)
            if w is not None:
                deps.add(w)
            rd = self.readers.get(r)
            if rd:
                deps.update(rd.values())
        if dma:
            self.ndma += 1
            op.key = ("dma", self.ndma)
            if chan is None:
                c = self.chan_rr
                self.chan_rr = (self.chan_rr + 1) % self.nchan
                ckey = ("rr", c)
            else:
                ckey = ("named", chan)
            st = self.named_chan.setdefault(ckey, [0, None])
            if st[1] is not None:
                deps.add(st[1])
            st[0] += 1
            st[1] = op
            op.sem = ckey
            op.val = 16 * st[0]
            op.flagged = True
        if eng == "pe":
            deps = {d for d in deps if d.is_dma or d.eng != "pe"}
        deps.discard(op)
        for d in deps:
            d.flagged = True
        op.deps = deps
        for r in reads:
            self.readers.setdefault(r, {})[op.key] = op
        for r in writes:
            self.last_w[r] = op
            self.readers[r] = {}
        self.ops[eng].append(op)
        return op


def build_program():
    nc = bass.Bass("TRN2", target_bir_lowering=False)

    def din(name, shape):
        return nc.dram_tensor(name, list(shape), F32, kind="ExternalInput").ap()

    def dout(name, shape):
        return nc.dram_tensor(name, list(shape), F32, kind="ExternalOutput").ap()

    xa_d = din("xa", [P, NCH, 640])
    xb_d = din("xb", [P, NCH, 576])
    hmask_d = din("hmask", [P, 1])
    pcorr_d = din("pcorr", [P, 4, 16])
    stb_d = din("st_b", [P, 2, 2, 8, 2])
    stc_d = din("st_c", [P, 2, 2, 8, 30])
    std_d = din("st_d", [P, 2, 2, 8, 15])
    gall_d = din("g_all", [P, 5, NCH])
    cconv_d = din("cconv", [P, 2, 8, 31])
    bconv_d = din("bconv", [P, 2, 8, 3])
    cvec_d = din("cvec", [P, 2, 4, 8])
    wsT_d = din("wsT", [P, 2, 8, 128])
    triu_d = din("triu", [P, 128])
    ab_d = din("ab", [P, 2, 8, 128])
    dwT_d = din("dwT", [P, 2, 4, 2, 256])
    cmat_d = din("cmat", [P, 2, 128])
    w_in_d = din("w_in", [2, 4096, 8192])
    w_out_d = din("w_out", [2, 4096, 4096])
    w_ff1_d = din("w_ff1", [2, 4096, 16384])
    w_ff2_d = din("w_ff2", [2, 16384, 4096])

    ya_d = dout("ya", [P, NCH, 512])
    yb_d = dout("yb", [P, NCH, 576])
    oc_d = dout("oc", [P, 2, 3, 8, 30])
    ob_d = dout("ob", [P, 2, 3, 8, 2])
    od_d = dout("od", [P, 2, 3, 8, 15])
    oav_d = dout("oav", [32, 2, 2, 1024])

    R = Rec()

    import contextlib
    es = contextlib.ExitStack()

    def sb(name, shape, dt):
        return es.enter_context(nc.sbuf_tensor(name, list(shape), dt))

    with es:
        x = sb("x", [P, NCH, 576], F32)
        h = sb("h", [P, NCH, 640], BF16)
        mix = sb("mix", [P, 16, 576], BF16)
        wring = sb("wring", [P, 3, 4096], BF16)
        tmp = sb("tmp", [P, NTMP, LPAD], F32)
        vtok = sb("vtok", [P, 5, 256], BF16)
        vtoks = sb("vtoks", [32, 2, 256], BF16)
        avs = sb("avs", [32, 2, 256], F32)
        dbf = sb("dbf", [P, 2, LPAD], BF16)
        tlc = sb("tlc", [P, 2, 8, 30], F32)
        tlb = sb("tlb", [P, 2, 8, 2], F32)
        tld = sb("tld", [P, 2, 8, 15], F32)
        stc = sb("stc", [P, 2, 8, 30], F32)
        stb = sb("stb", [P, 2, 8, 2], F32)
        std = sb("std", [P, 2, 8, 15], F32)
        gall = sb("gall", [P, 5, NCH], F32)
        cconv = sb("cconvs", [P, 2, 8, 31], F32)
        bconv = sb("bconvs", [P, 2, 8, 3], F32)
        cvec = sb("cvecs", [P, 2, 4, 8], F32)
        cmat = sb("cmats", [P, 2, 128], F32)
        onesb = sb("onesb", [P, 128], BF16)
        hmask = sb("hmasks", [P, 1], F32)
        pcorr = sb("pcorrs", [P, 4, 16], F32)
        triu = sb("trius", [P, 128], BF16)
        wmT = sb("wmT", [P, 8, 128], BF16)
        abT = sb("abT", [P, 8, 128], F32)
        dwb = sb("dwb", [P, 4, 2, 256], BF16)
        rstd = sb("rstd", [P, 640], F32)
        zeros = sb("zeros", [P, 32], F32)
        dg = sb("dg", [P, 2, 6, 128], BF16)
        glub = sb("glub", [P, 2, LPAD], BF16)
        cbc = sb("cbc", [P, 8], F32)
        ps = es.enter_context(nc.psum_tensor("ps", [P, 8, 512], F32))

        def act(fn, reads, writes):
            return R.add("act", fn, reads, writes)

        def dve(fn, reads, writes):
            return R.add("dve", fn, reads, writes)

        def pe(fn, reads, writes):
            return R.add("pe", fn, reads, writes)

        def dma(q, out, in_, reads, writes, chan=None):
            return R.add(q, lambda e, o=out, i=in_: e.dma_start(out=o, in_=i), reads, writes, dma=True, chan=chan)

        def ACT(out, in_, func, reads, writes, bias=None, scale=None):
            kw = {}
            if bias is not None:
                kw["bias"] = bias
            if scale is not None:
                kw["scale"] = scale
            return act(lambda e, o=out, i=in_, f=func, kw=kw: e.activation(out=o, in_=i, func=f, **kw), reads, writes)

        def TT(out, in0, in1, op, reads, writes):
            return dve(lambda e, o=out, a=in0, b=in1, op=op: e.tensor_tensor(out=o, in0=a, in1=b, op=op), reads, writes)

        def STT(out, in0, scalar, in1, op0, op1, reads, writes):
            return dve(lambda e, o=out, a=in0, s=scalar, b=in1, o0=op0, o1=op1:
                       e.scalar_tensor_tensor(out=o, in0=a, scalar=s, in1=b, op0=o0, op1=o1), reads, writes)

        def TS(out, in0, s1, s2, op0, op1, reads, writes):
            if op1 is None:
                return dve(lambda e, o=out, a=in0, s1=s1, o0=op0: e.tensor_scalar(out=o, in0=a, scalar1=s1, scalar2=None, op0=o0), reads, writes)
            return dve(lambda e, o=out, a=in0, s1=s1, s2=s2, o0=op0, o1=op1:
                       e.tensor_scalar(out=o, in0=a, scalar1=s1, scalar2=s2, op0=o0, op1=o1), reads, writes)

        def ACOPY(out, in_, reads, writes):
            return act(lambda e, o=out, i=in_: e.copy(out=o, in_=i), reads, writes)

        def RECIP(ap, reads, writes):
            return dve(lambda e, a=ap: e.reciprocal(out=a, in_=a), reads, writes)

        def MM(out, lhsT, rhs, start, stop, reads, writes):
            return pe(lambda e, o=out, l=lhsT, r=rhs, s=start, t=stop: e.matmul(o, l, r, start=s, stop=t), reads, writes)

        dma("sp", gall[:], gall_d, [], [("c", "gall")])
        dma("sp", cconv[:], cconv_d, [], [("c", "cconv")])
        dma("sp", bconv[:], bconv_d, [], [("c", "bconv")])
        dma("sp", cvec[:], cvec_d, [], [("c", "cvec")])
        dma("sp", cmat[:], cmat_d, [], [("c", "cmat")])
        dma("sp", hmask[:], hmask_d, [], [("c", "hmask")])
        dma("sp", pcorr[:], pcorr_d, [], [("c", "pcorr")])
        dma("pool", triu[:], triu_d, [], [("c", "triu")])
        dve(lambda e: e.memset(onesb[:], 1.0), [], [("c", "ones")])
        dve(lambda e: e.memset(zeros[:], 0.0), [], [("c", "zeros")])
        dve(lambda e: e.memset(tmp[:], 0.0), [], [("tmp", i) for i in range(NTMP)])
        dve(lambda e: e.memset(dbf[:], 0.0), [("dbf", 0), ("dbf", 1)], [("dbf", 0), ("dbf", 1)])

        wstate = {"slot": 0}

        wflat = wring[:, :, :].rearrange("p a b -> p (a b)")

        def wload(Wv, kc0, nk, c0, ncol):
            size = nk * ncol
            nsub = size // 2048
            j = wstate["slot"]
            if nsub == 2 and j % 2 == 1:
                j = (j + 1) % 6
            wstate["slot"] = (j + nsub) % 6
            view = wflat[:, j * 2048:j * 2048 + size].rearrange("p (k c) -> p k c", k=nk)
            regs = [("w", j + t) for t in range(nsub)]
            dma("pool", view, Wv[:, kc0:kc0 + nk, c0:c0 + ncol], [], regs, chan=("w", j))
            return regs, view

        pending = []
        epoch = {"e": 0}

        def push_job(fn):
            pending.append((epoch["e"], fn))

        def run_pending(n):
            for _ in range(n):
                if pending and pending[0][0] < epoch["e"]:
                    pending.pop(0)[1]()
            epoch["e"] += 1

        def flush_pending():
            while pending:
                pending.pop(0)[1]()

        def ntiles(t0, t1):
            n = (t1 - t0) // 2
            return [(t0, t0 + n), (t0 + n, t1)]

        def stream_fm(Wv, kchunks, col0, ncols, KB, CB, rhs_fn, rhs_reg, tiles, evac, bank0=0, kborder=None):
            nkb = len(kchunks) // KB
            nct = CB // 128
            kbs = list(range(nkb)) if kborder is None else kborder
            for cb in range(ncols // CB):
                c0 = col0 + cb * CB
                for bi, kb in enumerate(kbs):
                    regs, view = wload(Wv, kchunks[kb * KB], KB, c0, CB)
                    for ct in range(nct):
                        for kk in range(KB):
                            kidx = kb * KB + kk
                            for ti, (a, b) in enumerate(tiles):
                                bank = bank0 + ct * len(tiles) + ti
                                MM(ps[:, bank, 0:b - a], view[:, kk, ct * 128:(ct + 1) * 128], rhs_fn(kidx, a, b),
                                   bi == 0 and kk == 0, bi == nkb - 1 and kk == KB - 1,
                                   regs + [rhs_reg(kidx)], [("ps", bank)])
                    if bi < nkb - 1:
                        run_pending(1)
                for ct in range(nct):
                    banks = [bank0 + ct * len(tiles) + ti for ti in range(len(tiles))]
                    evac((c0 - col0) // 128 + ct, banks)
                run_pending(1)

        def norm_sq(ch, xap, xregion, ntok, slot, defer, first_of_buf):
            tl = ntiles(0, ntok) if ntok > 512 else [(0, ntok)]
            bi = slot // 2
            sqv = tmp[:, bi, :].bitcast(BF16)
            o_ = (slot % 2) * LPAD
            ACT(sqv[:, o_:o_ + ntok], xap, AF.Square, [xregion, ("tmp", bi)],
                [("sqs", slot)] + ([("tmp", bi)] if first_of_buf else []))

            def job(ch=ch):
                for ti, (a, b) in enumerate(tl):
                    MM(ps[:, 4 + ti, 0:b - a], onesb[:, :], sqv[:, o_ + a:o_ + b], ch == 0, ch == NCH - 1,
                       [("sqs", slot), ("tmp", bi), ("c", "ones")], [("ps", 4 + ti)])
            if defer:
                push_job(job)
            else:
                job()

        def norm_fin(xsrc, xreg, ntok, gi, hdst, hreg):
            flush_pending()
            tl = ntiles(0, ntok) if ntok > 512 else [(0, ntok)]
            for ti, (a, b) in enumerate(tl):
                ACT(rstd[:, a:b], ps[:, 4 + ti, 0:b - a], AF.Sqrt, [("ps", 4 + ti)], [("rstd", ti)],
                    bias=epsb[:, 0:1], scale=1.0 / 4096.0)
                RECIP(rstd[:, a:b], [("rstd", ti)], [("rstd", ti)])
            for ch in range(NCH):
                STT(hdst(ch), xsrc(ch), gall[:, gi, ch:ch + 1], rstd[:, 0:ntok], ALU.mult, ALU.mult,
                    [xreg(ch), ("c", "gall"), ("rstd", 0), ("rstd", 1)], [hreg(ch)])

        def rmsnorm(xsrc, xreg, ntok, gi, hdst, hreg):
            for ch in range(NCH):
                norm_sq(ch, xsrc(ch), xreg(ch), ntok, 10 + ch % 2, False, ch == 0)
            norm_fin(xsrc, xreg, ntok, gi, hdst, hreg)

        epsb = sb("epsb", [P, 1], F32)
        dve(lambda e: e.memset(epsb[:], EPS), [], [("c", "eps")])

        def run_pass(pi):
            if pi == 0:
                TH, XOFF = 640, 96
                segs_all = [(0, 640, 30)]
                nchunks = 5
            else:
                TH, XOFF = 576, 0
                segs_all = [(0, 512, 30), (512, 544, 572), (544, 576, 634)]
                nchunks = 4
            TX = TH - XOFF
            L = 670 if pi == 0 else 666
            Lc = L - 30

            if pi == 0:
                xfar = tmp[:, 0:5, :].rearrange("p a b -> p (a b)")[:, 0:3072].rearrange("p (c t) -> p c t", c=NCH)
                dma("sp", xfar, xa_d[:, :, 0:96], [], [("tmp", i) for i in range(5)])
                for q in range(4):
                    dma("sp", x[:, q * 8:(q + 1) * 8, 0:544], xa_d[:, q * 8:(q + 1) * 8, 96:640],
                        [], [("x", c) for c in range(q * 8, q * 8 + 8)])
            else:
                for q in range(4):
                    dma("sp", x[:, q * 8:(q + 1) * 8, 0:576], xb_d[:, q * 8:(q + 1) * 8, :],
                        [], [("x", c) for c in range(q * 8, q * 8 + 8)])

            for l in range(DEPTH):
                t0 = 0 if (pi == 1 or l == 0) else 96
                segs = [(max(ts, t0), te, pp + max(ts, t0) - ts) for (ts, te, pp) in segs_all]
                wtiles = ntiles(t0, TH)

                dma("pool", wmT[:], wsT_d[:, l], [], [("c", "wmT")])
                dve(lambda e: [e.tensor_tensor(out=wmT[:, hh, :], in0=wmT[:, hh, :], in1=triu[:, :], op=ALU.mult)
                               for hh in range(8)][-1], [("c", "wmT"), ("c", "triu")], [("c", "wmT")])
                dma("sp", abT[:], ab_d[:, l], [], [("c", "abT")])
                if pi == 1:
                    dma("sp", stc[:], stc_d[:, l], [], [("c", "stc")])
                    dma("sp", stb[:], stb_d[:, l], [], [("c", "stb")])
                    dma("sp", std[:], std_d[:, l], [], [("c", "std")])
                dma("pool", dwb[:], dwT_d[:, l], [], [("c", "dwb")])

                MM(ps[:, 7, 0:8], cmat[:, 0, :], cvec[:, l, 0, :], True, True, [("c", "cmat"), ("c", "cvec")], [("ps", 7)])
                ACT(cbc[:, :], ps[:, 7, 0:8], AF.Copy, [("ps", 7)], [("c", "cbc")])

                if pi == 0 and l == 0:
                    rmsnorm(lambda ch: xfar[:, ch, :], lambda ch: ("tmp", (ch * 96) // LPAD), 96, 0,
                            lambda ch: h[:, ch, 0:96], lambda ch: ("h", ch))
                if l == 0:
                    rmsnorm(lambda ch: x[:, ch, 0:TX], lambda ch: ("x", ch), TX, 2 * l,
                            lambda ch: h[:, ch, XOFF:TH], lambda ch: ("h", ch))
                else:
                    norm_fin(lambda ch: x[:, ch, 0:TX], lambda ch: ("x", ch), TX, 2 * l,
                             lambda ch: h[:, ch, XOFF:TH], lambda ch: ("h", ch))

                Win = w_in_d[l].rearrange("(kc p) n -> p kc n", p=P)
                Wout = w_out_d[l].rearrange("(kc p) n -> p kc n", p=P)
                W1 = w_ff1_d[l].rearrange("(kc p) n -> p kc n", p=P)
                W2 = w_ff2_d[l].rearrange("(kc p) n -> p kc n", p=P)
                allk = list(range(NCH))
                hrhs = lambda k, a, b: h[:, k, a:b]
                hreg = lambda k: ("h", k)

                def seg_pieces(a, b):
                    out = []
                    for (ts, te, pp) in segs:
                        lo, hi = max(a, ts), min(b, te)
                        if lo < hi:
                            out.append((lo, hi, pp + lo - ts))
                    return out

                def evac_to_pad(banks, dst_i, func, extra_reads=()):
                    for bank, (a, b) in zip(banks, wtiles):
                        for (lo, hi, pl) in seg_pieces(a, b):
                            ACT(tmp[:, dst_i, pl:pl + hi - lo], ps[:, bank, lo - a:hi - a], func,
                                [("ps", bank)] + list(extra_reads), [("tmp", dst_i)])

                def fill_hist(dst_i, nh, tl_t, st_t, ch):
                    if pi == 0:
                        ACOPY(tmp[:, dst_i, 30 - nh:30], zeros[:, 0:nh], [("c", "zeros")], [("tmp", dst_i)])
                    else:
                        ACOPY(tmp[:, dst_i, 30 - nh:30], tl_t[:, l, ch, :], [("tl", nh, l, ch)], [("tmp", dst_i)])
                        for sq_ in range(2):
                            pp = segs_all[1 + sq_][2]
                            ACOPY(tmp[:, dst_i, pp - nh:pp], st_t[:, sq_, ch, :],
                                  [("c", "stc"), ("c", "stb"), ("c", "std")], [("tmp", dst_i)])

                def save_tails(src_i, nh, tl_t, out_d, ch):
                    if pi == 0:
                        ACOPY(tl_t[:, l, ch, :], tmp[:, src_i, 670 - nh:670], [("tmp", src_i)], [("tl", nh, l, ch)])
                    else:
                        for si, (ts, te, pp) in enumerate(segs_all):
                            pe_ = pp + te - ts
                            dma("sp", out_d[:, l, si, ch, :], tmp[:, src_i, pe_ - nh:pe_], [("tmp", src_i)], [])

                def pad_to_mix(src_ap_fn, mt, emit):
                    for (ts, te, pp) in segs_all:
                        lo = max(ts, XOFF)
                        if lo >= te:
                            continue
                        j0 = pp + (lo - ts) - 30
                        emit(mix[:, mt, lo - XOFF:te - XOFF], src_ap_fn(j0, j0 + te - lo))

                for i in range(4):
                    def evac_cg(ct, banks, i=i):
                        p_ = ct % 2
                        evac_to_pad(banks, p_, AF.Sigmoid)
                    stream_fm(Win, allk, 6144 + 256 * i, 256, 8, 256, hrhs, hreg, wtiles, evac_cg)

                    def evac_ca(ct, banks, i=i):
                        p_ = ct % 2
                        ch = 2 * i + p_
                        S, G, Cn, Sq = p_, 2, 3 + p_, 5
                        if p_ == 0:
                            flush_pending()
                        evac_to_pad(banks, G, AF.Copy)
                        TT(tmp[:, G, 30:L], tmp[:, G, 30:L], tmp[:, S, 30:L], ALU.mult, [("tmp", G), ("tmp", S)], [("tmp", G)])
                        fill_hist(G, 30, tlc, stc, ch)
                        save_tails(G, 30, tlc, oc_d, ch)
                        dve(lambda e, o=glub[:, p_, 0:L], a=tmp[:, G, 0:L]: e.tensor_copy(out=o, in_=a), [("tmp", G)], [("glub", p_)])
                        ct2 = ntiles(0, Lc)

                        def build_dg(tg, ch=ch):
                            for j, k in enumerate(range(6 * tg, min(6 * tg + 6, 31))):
                                TS(dg[:, tg % 2, j, :], cmat[:, 0, :], cconv[:, l, ch, k:k + 1], None, ALU.mult, None,
                                   [("c", "cmat"), ("c", "cconv")], [("dg", tg % 2)])

                        def job_diag():
                            build_dg(0)
                            build_dg(1)

                        def job_conv(ch=ch, p_=p_):
                            for tg in range(6):
                                taps = list(range(6 * tg, min(6 * tg + 6, 31)))
                                if tg >= 2:
                                    build_dg(tg)
                                for j, k in enumerate(taps):
                                    for ti, (a, b) in enumerate(ct2):
                                        MM(ps[:, 4 + ti, 0:b - a], dg[:, tg % 2, j, :], glub[:, p_, k + a:k + b], k == 0, k == 30,
                                           [("dg", tg % 2), ("glub", p_)], [("ps", 4 + ti)])
                            for ti, (a, b) in enumerate(ct2):
                                ACT(tmp[:, Cn, a:b], ps[:, 4 + ti, 0:b - a], AF.Identity, [("ps", 4 + ti), ("c", "cbc")], [("tmp", Cn)],
                                    bias=cbc[:, ch:ch + 1])
                                ACT(tmp[:, Sq, a:b], ps[:, 4 + ti, 0:b - a], AF.Square, [("ps", 4 + ti), ("c", "cbc")], [("tmp", Sq)],
                                    bias=cbc[:, ch:ch + 1])

                        def job_var(ch=ch, p_=p_):
                            for ti, (a, b) in enumerate(ct2):
                                MM(ps[:, 6 + ti, 0:b - a], cmat[:, 1, :], tmp[:, Sq, a:b], True, True,
                                   [("c", "cmat"), ("tmp", Sq)], [("ps", 6 + ti)])
                            for ti, (a, b) in enumerate(ct2):
                                ACT(tmp[:, Sq, a:b], ps[:, 6 + ti, 0:b - a], AF.Sqrt, [("ps", 6 + ti)], [("tmp", Sq)],
                                    bias=epsb[:, 0:1], scale=1.0)
                            RECIP(tmp[:, Sq, 0:Lc], [("tmp", Sq)], [("tmp", Sq)])
                            TT(tmp[:, Cn, 0:Lc], tmp[:, Cn, 0:Lc], tmp[:, Sq, 0:Lc], ALU.mult, [("tmp", Cn), ("tmp", Sq)], [("tmp", Cn)])
                            pad_to_mix(lambda j0, j1: tmp[:, Cn, j0:j1], ch,
                                       lambda mo, so: ACT(mo, so, AF.Silu, [("tmp", Cn), ("c", "cvec")], [("mix", ch)],
                                                          bias=cvec[:, l, 2, ch:ch + 1], scale=cvec[:, l, 1, ch:ch + 1]))
                        push_job(job_diag)
                        push_job(job_conv)
                        push_job(job_var)
                    stream_fm(Win, allk, 5120 + 256 * i, 256, 8, 256, hrhs, hreg, wtiles, evac_ca)

                for g in range(4):
                    def evac_dp(ct, banks, g=g):
                        p_ = ct % 2
                        ch = 2 * g + p_
                        D0, A1, A2 = p_, 2 + 2 * p_, 3 + 2 * p_
                        if p_ == 0:
                            flush_pending()
                        evac_to_pad(banks, D0, AF.Copy)
                        fill_hist(D0, 15, tld, std, ch)
                        save_tails(D0, 15, tld, od_d, ch)
                        w = POOLW[g]
                        src, vs, step = D0, 15, 1
                        pp_ = [A1, A2]
                        k = 0
                        while step < w:
                            dst = pp_[k % 2]
                            nv = vs + step
                            TT(tmp[:, dst, nv:L], tmp[:, src, nv:L], tmp[:, src, nv - step:L - step], ALU.add,
                               [("tmp", src)], [("tmp", dst)])
                            src, vs, step, k = dst, nv, step * 2, k + 1
                        if pi == 0:
                            TT(tmp[:, src, 158:174], tmp[:, src, 158:174], pcorr[:, g, :], ALU.mult,
                               [("tmp", src), ("c", "pcorr")], [("tmp", src)])
                        STT(dbf[:, p_, 30:L], tmp[:, src, 30:L], 1.0 / w, tmp[:, D0, 30:L], ALU.mult, ALU.subtract,
                            [("tmp", src), ("tmp", D0)], [("dbf", p_)])
                        def job_dmap(g=g):
                            dtl = ntiles(30, L)
                            for dt_ in range(2):
                                for kc in range(2):
                                    for ti, (a, b) in enumerate(dtl):
                                        bank = 4 + dt_ * 2 + ti
                                        MM(ps[:, bank, 0:b - a], dwb[:, g, kc, dt_ * 128:(dt_ + 1) * 128], dbf[:, kc, a:b],
                                           kc == 0, kc == 1, [("c", "dwb"), ("dbf", kc)], [("ps", bank)])
                            for dt_ in range(2):
                                chd = 2 * g + dt_
                                for ti, (a, b) in enumerate(dtl):
                                    bank = 4 + dt_ * 2 + ti
                                    for (ts, te, pp) in segs_all:
                                        lo_t = max(ts, XOFF)
                                        if lo_t >= te:
                                            continue
                                        plo, phi = pp + lo_t - ts, pp + te - ts
                                        qlo, qhi = max(plo, a), min(phi, b)
                                        if qlo >= qhi:
                                            continue
                                        tlo = lo_t + (qlo - plo)
                                        ACT(mix[:, 8 + chd, tlo - XOFF:tlo - XOFF + qhi - qlo], ps[:, bank, qlo - a:qhi - a],
                                            AF.Copy, [("ps", bank), ("c", "cvec")], [("mix", 8 + chd)],
                                            scale=cvec[:, l, 3, chd:chd + 1])
                        if p_ == 1:
                            push_job(job_dmap)
                    stream_fm(Win, allk, 7168 + 256 * g, 256, 8, 256, hrhs, hreg, wtiles, evac_dp)
                def b_first(i):
                    def evac_bh(ct, banks):
                        evac_to_pad(banks, 2 * (ct % 2), AF.Copy)
                    stream_fm(Win, allk, 2048 + 256 * i, 256, 8, 256, hrhs, hreg, wtiles, evac_bh)

                    def evac_bc(ct, banks, i=i):
                        p_ = ct % 2
                        ch = 2 * i + p_
                        H, Cb = 2 * p_, 2 * p_ + 1
                        evac_to_pad(banks, Cb, AF.Copy)
                        TT(tmp[:, Cb, 30:L], tmp[:, Cb, 30:L], tmp[:, H, 30:L], ALU.mult, [("tmp", Cb), ("tmp", H)], [("tmp", Cb)])
                        fill_hist(Cb, 2, tlb, stb, ch)
                        save_tails(Cb, 2, tlb, ob_d, ch)
                        TS(tmp[:, H, 0:Lc], tmp[:, Cb, 28:28 + Lc], bconv[:, l, ch, 0:1], None, ALU.mult, None,
                           [("tmp", Cb), ("c", "bconv")], [("tmp", H)])
                        for k in (1, 2):
                            STT(tmp[:, H, 0:Lc], tmp[:, Cb, 28 + k:28 + k + Lc], bconv[:, l, ch, k:k + 1], tmp[:, H, 0:Lc],
                                ALU.mult, ALU.add, [("tmp", Cb), ("tmp", H), ("c", "bconv")], [("tmp", H)])
                    stream_fm(Win, allk, 4096 + 256 * i, 256, 8, 256, hrhs, hreg, wtiles, evac_bc)

                def b_second(i):
                    def evac_bb(ct, banks, i=i):
                        p_ = ct % 2
                        ch = 2 * i + p_
                        H, Cb = 2 * p_, 2 * p_ + 1
                        evac_to_pad(banks, Cb, AF.Copy)
                        for (ts, te, pp) in segs_all:
                            lo = max(ts, XOFF)
                            if lo >= te:
                                continue
                            p0 = pp + lo - ts
                            n = te - lo
                            TT(mix[:, 8 + ch, lo - XOFF:te - XOFF], tmp[:, Cb, p0:p0 + n], tmp[:, H, p0 - 30:p0 - 30 + n], ALU.mult,
                               [("tmp", Cb), ("tmp", H)], [("mix", 8 + ch)])
                    stream_fm(Win, allk, 3072 + 256 * i, 256, 8, 256, hrhs, hreg, wtiles, evac_bb)

                b_first(0)
                flush_pending()
                xtiles = ntiles(32 if (pi == 0 and l == 1) else 0, TX)

                def evac_x(ct, banks):
                    for bank, (a, b) in zip(banks, xtiles):
                        TT(x[:, ct, a:b], ps[:, bank, 0:b - a], x[:, ct, a:b], ALU.add, [("ps", bank), ("x", ct)], [("x", ct)])

                stream_fm(Wout, list(range(16, 32)), 0, 4096, 8, 256, lambda k, a, b: mix[:, k, a:b],
                          lambda k: ("mix", k), xtiles, evac_x)

                b_second(0)
                for i in range(1, 4):
                    b_first(i)
                    b_second(i)

                for i in range(4):
                    c0 = 1024 + 256 * i
                    views = [wload(Win, 8 * q_, 8, c0, 256) for q_ in range(4)]
                    jobs = [(tc * 128, tc * 128 + 128, tc, None) for tc in range(nchunks)]
                    if pi == 1:
                        jobs += [(512, 544, None, 0), (544, 576, None, 1)]
                    avbanks = [0, 1, 2, 3, 6, 7]
                    for kq in range(4):
                        regs_, v_ = views[kq]
                        for ji, (a, b, tc, sq_) in enumerate(jobs):
                            bank = avbanks[ji]
                            m = b - a
                            for kk in range(8):
                                kc = kq * 8 + kk
                                MM(ps[0:m, bank, 0:256], h[:, kc, a:b], v_[:, kk, :], kc == 0, kc == NCH - 1,
                                   regs_ + [("h", kc)], [("ps", bank)])
                        if kq == 0:
                            flush_pending()
                    for ji, (a, b, tc, sq_) in enumerate(jobs):
                        bank = avbanks[ji]
                        if tc is not None:
                            ACT(vtok[:, tc, :], ps[:, bank, 0:256], AF.Gelu, [("ps", bank)], [("vtok", tc)])
                        else:
                            ACT(avs[0:32, sq_, :], ps[0:32, bank, 0:256], AF.Gelu, [("ps", bank)], [("avs", sq_)])
                            ACOPY(vtoks[0:32, sq_, :], avs[0:32, sq_, :], [("avs", sq_)], [("vtoks", sq_)])
                            dma("sp", oav_d[0:32, l, sq_, 256 * i:256 * i + 256], avs[0:32, sq_, :], [("avs", sq_)], [])

                    def evac_au(ct, banks, i=i):
                        hd = 2 * i + ct
                        for bank, (a, b) in zip(banks, wtiles):
                            lo = max(a, XOFF)
                            if lo < b:
                                ACT(mix[:, hd, lo - XOFF:b - XOFF], ps[:, bank, lo - a:b - a], AF.Gelu, [("ps", bank)], [("mix", hd)])
                    stream_fm(Win, allk, 256 * i, 256, 8, 256, hrhs, hreg, wtiles, evac_au)

                    def job_gate(i=i):
                        for hh in range(2):
                            hd = 2 * i + hh
                            gj = []
                            for tc in range(nchunks):
                                bank, cc = (4, tc * 128) if tc < 4 else (5, 0)
                                MM(ps[:, bank, cc:cc + 128], vtok[:, tc, hh * 128:(hh + 1) * 128], wmT[:, hd, :], True, True,
                                   [("vtok", tc), ("c", "wmT")], [("ps", bank)])
                                gj.append((bank, cc, 128, tc * 128, 0))
                            if pi == 1:
                                for sq_ in range(2):
                                    MM(ps[:, 5, 32 * sq_:32 * sq_ + 32], vtoks[0:32, sq_, hh * 128:(hh + 1) * 128], wmT[0:32, hd, 0:32],
                                       True, True, [("vtoks", sq_), ("c", "wmT")], [("ps", 5)])
                                    gj.append((5, 32 * sq_, 32, 512 + 32 * sq_, 0))
                            for (bank, cc, n, tlo, bl) in gj:
                                lo = max(tlo, XOFF)
                                if lo >= tlo + n:
                                    continue
                                sk = lo - tlo
                                n2 = n - sk
                                gt = tmp[:, 4 + hh, 0:n2]
                                TT(gt, ps[:, bank, cc + sk:cc + n], abT[:, hd, bl + sk:bl + n], ALU.add,
                                   [("ps", bank), ("c", "abT")], [("tmp", 4 + hh)])
                                TT(mix[:, hd, lo - XOFF:lo - XOFF + n2], gt, mix[:, hd, lo - XOFF:lo - XOFF + n2], ALU.mult,
                                   [("tmp", 4 + hh), ("mix", hd)], [("mix", hd)])
                    push_job(job_gate)
                flush_pending()

                def evac_x_stats(ct, banks, mask=False):
                    evac_x(ct, banks)
                    if mask:
                        TS(x[:, ct, 0:32], x[:, ct, 0:32], hmask[:, 0:1], None, ALU.mult, None,
                           [("x", ct), ("c", "hmask")], [("x", ct)])
                    norm_sq(ct, x[:, ct, 0:TX], ("x", ct), TX, ct % 12, True, ct < 12 and ct % 2 == 0)

                stream_fm(Wout, list(range(0, 16)), 0, 4096, 8, 256, lambda k, a, b: mix[:, k, a:b],
                          lambda k: ("mix", k), xtiles, evac_x_stats, kborder=[1, 0])

                norm_fin(lambda ch: x[:, ch, 0:TX], lambda ch: ("x", ch), TX, 2 * l + 1,
                         lambda ch: h[:, ch, XOFF:TH], lambda ch: ("h", ch))
                hx = lambda k, a, b: h[:, k, XOFF + a:XOFF + b]
                rot = {"i": 0}
                for grp in range(8):
                    def evac_f1(ct, banks, grp=grp):
                        for bank, (a, b) in zip(banks, xtiles):
                            ti_ = rot["i"] % 4
                            rot["i"] += 1
                            ACT(tmp[:, ti_, 0:b - a], ps[:, bank, 0:b - a], AF.Relu, [("ps", bank)], [("tmp", ti_)])
                            TT(mix[:, ct, a:b], tmp[:, ti_, 0:b - a], tmp[:, ti_, 0:b - a], ALU.mult, [("tmp", ti_)], [("mix", ct)])
                    stream_fm(W1, allk, grp * 2048, 2048, 8, 512, hx, hreg, xtiles, evac_f1)
                    if grp < 7:
                        stream_fm(W2, list(range(grp * 16, grp * 16 + 16)), 0, 4096, 8, 512, lambda k, a, b: mix[:, k, a:b],
                                  lambda k: ("mix", k), xtiles, evac_x)
                    else:
                        stream_fm(W2, list(range(grp * 16, grp * 16 + 16)), 0, 4096, 8, 256, lambda k, a, b: mix[:, k, a:b],
                                  lambda k: ("mix", k), xtiles,
                                  lambda ct, banks: evac_x_stats(ct, banks, mask=(pi == 0 and l == 0)))

            norm_fin(lambda ch: x[:, ch, 0:TX], lambda ch: ("x", ch), TX, 4,
                     lambda ch: x[:, ch, 0:TX], lambda ch: ("x", ch))
            for q in range(4):
                regs = [("x", c) for c in range(q * 8, q * 8 + 8)]
                if pi == 0:
                    dma("sp", ya_d[:, q * 8:(q + 1) * 8, :], x[:, q * 8:(q + 1) * 8, 32:544], regs, [])
                else:
                    dma("sp", yb_d[:, q * 8:(q + 1) * 8, :], x[:, q * 8:(q + 1) * 8, 0:576], regs, [])

        run_pass(0)
        run_pass(1)

        sem_names = set()
        for e in Rec.ENGS:
            cnt = 0
            for op in R.ops[e]:
                if op.is_dma:
                    sem_names.add(op.sem)
                elif op.flagged:
                    cnt += 1
                    op.val = cnt
                    op.sem = ("prog", e)
            sem_names.add(("prog", e))
        sem_names = sorted(sem_names, key=str)
        sems = {}
        for i, k in enumerate(sem_names):
            sems[k] = es.enter_context(nc.semaphore("s%d" % i))

        out_dmas = [op for e in Rec.ENGS for op in R.ops[e] if op.is_dma]

        def emit(engobj, ename):
            waited = {}
            for op in R.ops[ename]:
                need = {}
                for d in op.deps:
                    if d.val is None:
                        continue
                    if need.get(d.sem, 0) < d.val:
                        need[d.sem] = d.val
                for k, v in need.items():
                    if waited.get(k, 0) < v:
                        engobj.wait_ge(sems[k], v)
                        waited[k] = v
                ins = op.fn(engobj)
                if op.is_dma:
                    ins.then_inc(sems[op.sem], 16)
                elif op.flagged:
                    ins.then_inc(sems[op.sem], 1)
            if ename == "sp":
                fin = {}
                for op in out_dmas:
                    if fin.get(op.sem, 0) < op.val:
                        fin[op.sem] = op.val
                for k, v in fin.items():
                    engobj.wait_ge(sems[k], v)

        with nc.Block() as block:
            @block.tensor
            def _(e):
                emit(e, "pe")

            @block.scalar
            def _(e):
                emit(e, "act")

            @block.vector
            def _(e):
                emit(e, "dve")

            @block.gpsimd
            def _(e):
                emit(e, "pool")

            @block.sync
            def _(e):
                emit(e, "sp")
    return nc


def _fm(a):
    T, C = a.shape
    return np.ascontiguousarray(a.reshape(T, C // 128, 128).transpose(2, 1, 0))


def _fm_inv(a):
    p, n, T = a.shape
    return np.ascontiguousarray(a.transpose(2, 1, 0).reshape(T, n * 128))


_NC_CACHE = {}


def kernel(x_prompt, x_sample, state_conv_b, state_conv_c, state_pool, g_mix, w_in, a_ws, a_b, b_conv,
           c_conv, c_conv_b, c_ln_g, c_ln_b, d_w, d_scale, w_out, g_ffn, w_ff1, w_ff2, g_final):
    f = lambda a: np.ascontiguousarray(np.asarray(a, dtype=np.float32))
    x_prompt, x_sample = f(x_prompt), f(x_sample)
    state_conv_b, state_conv_c, state_pool = f(state_conv_b), f(state_conv_c), f(state_pool)
    w_in, w_out, w_ff1, w_ff2 = f(w_in), f(w_out), f(w_ff1), f(w_ff2)
    g_mix, g_ffn, g_final = f(g_mix), f(g_ffn), f(g_final)
    a_ws, a_b, b_conv, c_conv = f(a_ws), f(a_b), f(b_conv), f(c_conv)
    c_conv_b, c_ln_g, c_ln_b, d_w, d_scale = f(c_conv_b), f(c_ln_g), f(c_ln_b), f(d_w), f(d_scale)

    def vec_fm(v, nchunk):
        return v.reshape(nchunk, 128).T

    g_all = np.ascontiguousarray(np.stack([vec_fm(g_mix[0], 32), vec_fm(g_ffn[0], 32), vec_fm(g_mix[1], 32),
                                           vec_fm(g_ffn[1], 32), vec_fm(g_final, 32)], axis=1))
    cconv = np.ascontiguousarray(c_conv.reshape(2, 31, 8, 128).transpose(3, 0, 2, 1))
    bconv = np.ascontiguousarray(b_conv.reshape(2, 3, 8, 128).transpose(3, 0, 2, 1))
    cvec = np.ascontiguousarray(np.stack([c_conv_b, c_ln_g, c_ln_b, d_scale], axis=1).reshape(2, 4, 8, 128).transpose(3, 0, 1, 2))
    wsT = np.ascontiguousarray(a_ws.transpose(3, 0, 1, 2))
    jj, ii = np.meshgrid(np.arange(128), np.arange(128), indexing="ij")
    triu = np.ascontiguousarray((jj <= ii).astype(np.float32))
    ab = np.ascontiguousarray(np.broadcast_to(a_b[None], (128, 2, 8, 128)))
    dwT = np.ascontiguousarray(d_w.reshape(2, 4, 2, 128, 256).transpose(3, 0, 1, 2, 4))
    cm = np.stack([np.eye(128, dtype=np.float32) - 1.0 / 128.0, np.full((128, 128), 1.0 / 128.0, np.float32)], axis=1)
    cmat = np.ascontiguousarray(cm.astype(np.float32))

    in_maps = []
    for c in range(8):
        b, k = c // 4, c % 4
        s = 1024 * k
        xa = np.zeros((640, 4096), np.float32)
        if k > 0:
            xa[:] = x_prompt[b, s - 128:s + 512]
        else:
            xa[128:] = x_prompt[b, 0:512]
        xb = np.concatenate([x_prompt[b, s + 512:s + 1024], x_sample[2 * c], x_sample[2 * c + 1]], axis=0)
        pc = np.ones((128, 4, 16), np.float32)
        if k == 0:
            for g, w in enumerate(POOLW):
                for t in range(16):
                    pc[:, g, t] = np.float32(w) / np.float32(min(t + 1, w))
        sq = [2 * c, 2 * c + 1]

        def st_fm(st, nr):
            a = st[:, sq]
            return np.ascontiguousarray(a.reshape(2, 2, nr, 8, 128).transpose(4, 0, 1, 3, 2))
        in_maps.append({
            "xa": _fm(xa), "xb": _fm(xb),
            "hmask": np.full((128, 1), 0.0 if k == 0 else 1.0, np.float32),
            "pcorr": pc,
            "st_b": st_fm(state_conv_b, 2), "st_c": st_fm(state_conv_c, 30), "st_d": st_fm(state_pool, 15),
            "g_all": g_all, "cconv": cconv, "bconv": bconv, "cvec": cvec, "wsT": wsT, "triu": triu, "ab": ab,
            "dwT": dwT, "cmat": cmat, "w_in": w_in, "w_out": w_out, "w_ff1": w_ff1, "w_ff2": w_ff2,
        })

    if "nc" not in _NC_CACHE:
        _NC_CACHE["nc"] = build_program()
    nc = _NC_CACHE["nc"]
    res = run_bass_kernel_spmd(nc, in_maps, core_ids=list(range(8)))
    outs = res.results

    y_prompt = np.zeros((2, 4096, 4096), np.float32)
    y_sample = np.zeros((16, 32, 4096), np.float32)
    ncb_p = np.zeros((2, 2, 2, 1024), np.float32)
    ncc_p = np.zeros((2, 2, 30, 1024), np.float32)
    npl_p = np.zeros((2, 2, 15, 1024), np.float32)
    ncb_s = np.zeros((2, 16, 2, 1024), np.float32)
    ncc_s = np.zeros((2, 16, 30, 1024), np.float32)
    npl_s = np.zeros((2, 16, 15, 1024), np.float32)
    nav_s = np.zeros((2, 16, 32, 1024), np.float32)

    def tail_inv(a):
        return a.transpose(2, 1, 0).reshape(a.shape[2], 1024)

    for c in range(8):
        b, k = c // 4, c % 4
        s = 1024 * k
        o = outs[c]
        y_prompt[b, s:s + 512] = _fm_inv(o["ya"])
        yb = _fm_inv(o["yb"])
        y_prompt[b, s + 512:s + 1024] = yb[0:512]
        y_sample[2 * c] = yb[512:544]
        y_sample[2 * c + 1] = yb[544:576]
        for l in range(2):
            for si in range(3):
                tb, tc_, td = tail_inv(o["ob"][:, l, si]), tail_inv(o["oc"][:, l, si]), tail_inv(o["od"][:, l, si])
                if si == 0:
                    if k == 3:
                        ncb_p[l, b], ncc_p[l, b], npl_p[l, b] = tb, tc_, td
                else:
                    q = 2 * c + si - 1
                    ncb_s[l, q], ncc_s[l, q], npl_s[l, q] = tb, tc_, td
            for si in range(2):
                nav_s[l, 2 * c + si] = o["oav"][:, l, si, :]
    return (y_prompt, y_sample, ncb_p, ncc_p, npl_p, ncb_s, ncc_s, npl_s, nav_s)
```

```python
import numpy as np
import concourse.bass as bass
import concourse.mybir as mybir
from concourse.bass_utils import run_bass_kernel_spmd

F32 = mybir.dt.float32
BF16 = mybir.dt.bfloat16
AF = mybir.ActivationFunctionType
ALU = mybir.AluOpType

P = 128
NCH = 32
DEPTH = 2
EPS = 1e-6
NTMP = 6
LPAD = 672
POOLW = (2, 4, 8, 16)


class Op:
    __slots__ = ("eng", "fn", "deps", "flagged", "val", "sem", "is_dma", "key")

    def __init__(self, eng, fn, is_dma=False):
        self.eng = eng
        self.fn = fn
        self.deps = ()
        self.flagged = False
        self.val = None
        self.sem = None
        self.is_dma = is_dma
        self.key = eng


class Rec:
    ENGS = ("pe", "act", "dve", "pool", "sp")

    def __init__(self):
        self.ops = {e: [] for e in self.ENGS}
        self.last_w = {}
        self.readers = {}
        self.nchan = 8
        self.chan_use = [0] * self.nchan
        self.chan_last = [None] * self.nchan
        self.chan_rr = 0
        self.named_chan = {}
        self.ndma = 0

    def add(self, eng, fn, reads=(), writes=(), dma=False, chan=None):
        op = Op(eng, fn, dma)
        deps = set()
        for r in reads:
            w = self.last_w.get(r)
            if w is not None:
                deps.add(w)
        for r in writes:
            w = self.last_w.get(r)
            if w is not None:
                deps.add(w)
            rd = self.readers.get(r)
            if rd:
                deps.update(rd.values())
        if dma:
            self.ndma += 1
            op.key = ("dma", self.ndma)
            if chan is None:
                c = self.chan_rr
                self.chan_rr = (self.chan_rr + 1) % self.nchan
                ckey = ("rr", c)
            else:
                ckey = ("named", chan)
            st = self.named_chan.setdefault(ckey, [0, None])
            if st[1] is not None:
                deps.add(st[1])
            st[0] += 1
            st[1] = op
            op.sem = ckey
            op.val = 16 * st[0]
            op.flagged = True
        if eng == "pe":
            deps = {d for d in deps if d.is_dma or d.eng != "pe"}
        deps.discard(op)
        for d in deps:
            d.flagged = True
        op.deps = deps
        for r in reads:
            self.readers.setdefault(r, {})[op.key] = op
        for r in writes:
            self.last_w[r] = op
            self.readers[r] = {}
        self.ops[eng].append(op)
        return op


def build_program():
    nc = bass.Bass("TRN2", target_bir_lowering=False)

    def din(name, shape):
        return nc.dram_tensor(name, list(shape), F32, kind="ExternalInput").ap()

    def dout(name, shape):
        return nc.dram_tensor(name, list(shape), F32, kind="ExternalOutput").ap()

    xa_d = din("xa", [P, NCH, 640])
    xb_d = din("xb", [P, NCH, 576])
    hmask_d = din("hmask", [P, 1])
    pcorr_d = din("pcorr", [P, 4, 16])
    stb_d = din("st_b", [P, 2, 2, 8, 2])
    stc_d = din("st_c", [P, 2, 2, 8, 30])
    std_d = din("st_d", [P, 2, 2, 8, 15])
    gall_d = din("g_all", [P, 5, NCH])
    cconv_d = din("cconv", [P, 2, 8, 31])
    bconv_d = din("bconv", [P, 2, 8, 3])
    cvec_d = din("cvec", [P, 2, 4, 8])
    wsT_d = din("wsT", [P, 2, 8, 128])
    triu_d = din("triu", [P, 128])
    ab_d = din("ab", [P, 2, 8, 128])
    dwT_d = din("dwT", [P, 2, 4, 2, 256])
    cmat_d = din("cmat", [P, 2, 128])
    w_in_d = din("w_in", [2, 4096, 8192])
    w_out_d = din("w_out", [2, 4096, 4096])
    w_ff1_d = din("w_ff1", [2, 4096, 16384])
    w_ff2_d = din("w_ff2", [2, 16384, 4096])

    ya_d = dout("ya", [P, NCH, 512])
    yb_d = dout("yb", [P, NCH, 576])
    oc_d = dout("oc", [P, 2, 3, 8, 30])
    ob_d = dout("ob", [P, 2, 3, 8, 2])
    od_d = dout("od", [P, 2, 3, 8, 15])
    oav_d = dout("oav", [32, 2, 2, 1024])

    R = Rec()

    import contextlib
    es = contextlib.ExitStack()

    def sb(name, shape, dt):
        return es.enter_context(nc.sbuf_tensor(name, list(shape), dt))

    with es:
        x = sb("x", [P, NCH, 576], F32)
        h = sb("h", [P, NCH, 640], BF16)
        mix = sb("mix", [P, 16, 576], BF16)
        wring = sb("wring", [P, 3, 4096], BF16)
        tmp = sb("tmp", [P, NTMP, LPAD], F32)
        vtok = sb("vtok", [P, 5, 256], BF16)
        vtoks = sb("vtoks", [32, 2, 256], BF16)
        avs = sb("avs", [32, 2, 256], F32)
        dbf = sb("dbf", [P, 2, LPAD], BF16)
        tlc = sb("tlc", [P, 2, 8, 30], F32)
        tlb = sb("tlb", [P, 2, 8, 2], F32)
        tld = sb("tld", [P, 2, 8, 15], F32)
        stc = sb("stc", [P, 2, 8, 30], F32)
        stb = sb("stb", [P, 2, 8, 2], F32)
        std = sb("std", [P, 2, 8, 15], F32)
        gall = sb("gall", [P, 5, NCH], F32)
        cconv = sb("cconvs", [P, 2, 8, 31], F32)
        bconv = sb("bconvs", [P, 2, 8, 3], F32)
        cvec = sb("cvecs", [P, 2, 4, 8], F32)
        cmat = sb("cmats", [P, 2, 128], F32)
        onesb = sb("onesb", [P, 128], BF16)
        hmask = sb("hmasks", [P, 1], F32)
        pcorr = sb("pcorrs", [P, 4, 16], F32)
        triu = sb("trius", [P, 128], BF16)
        wmT = sb("wmT", [P, 8, 128], BF16)
        abT = sb("abT", [P, 8, 128], F32)
        dwb = sb("dwb", [P, 4, 2, 256], BF16)
        rstd = sb("rstd", [P, 640], F32)
        zeros = sb("zeros", [P, 32], F32)
        dg = sb("dg", [P, 2, 6, 128], BF16)
        glub = sb("glub", [P, 2, LPAD], BF16)
        cbc = sb("cbc", [P, 8], F32)
        ps = es.enter_context(nc.psum_tensor("ps", [P, 8, 512], F32))

        def act(fn, reads, writes):
            return R.add("act", fn, reads, writes)

        def dve(fn, reads, writes):
            return R.add("dve", fn, reads, writes)

        def pe(fn, reads, writes):
            return R.add("pe", fn, reads, writes)

        def dma(q, out, in_, reads, writes, chan=None):
            return R.add(q, lambda e, o=out, i=in_: e.dma_start(out=o, in_=i), reads, writes, dma=True, chan=chan)

        def ACT(out, in_, func, reads, writes, bias=None, scale=None):
            kw = {}
            if bias is not None:
                kw["bias"] = bias
            if scale is not None:
                kw["scale"] = scale
            return act(lambda e, o=out, i=in_, f=func, kw=kw: e.activation(out=o, in_=i, func=f, **kw), reads, writes)

        def TT(out, in0, in1, op, reads, writes):
            return dve(lambda e, o=out, a=in0, b=in1, op=op: e.tensor_tensor(out=o, in0=a, in1=b, op=op), reads, writes)

        def STT(out, in0, scalar, in1, op0, op1, reads, writes):
            return dve(lambda e, o=out, a=in0, s=scalar, b=in1, o0=op0, o1=op1:
                       e.scalar_tensor_tensor(out=o, in0=a, scalar=s, in1=b, op0=o0, op1=o1), reads, writes)

        def TS(out, in0, s1, s2, op0, op1, reads, writes):
            if op1 is None:
                return dve(lambda e, o=out, a=in0, s1=s1, o0=op0: e.tensor_scalar(out=o, in0=a, scalar1=s1, scalar2=None, op0=o0), reads, writes)
            return dve(lambda e, o=out, a=in0, s1=s1, s2=s2, o0=op0, o1=op1:
                       e.tensor_scalar(out=o, in0=a, scalar1=s1, scalar2=s2, op0=o0, op1=o1), reads, writes)

        def ACOPY(out, in_, reads, writes):
            return act(lambda e, o=out, i=in_: e.copy(out=o, in_=i), reads, writes)

        def RECIP(ap, reads, writes):
            return dve(lambda e, a=ap: e.reciprocal(out=a, in_=a), reads, writes)

        def MM(out, lhsT, rhs, start, stop, reads, writes):
            return pe(lambda e, o=out, l=lhsT, r=rhs, s=start, t=stop: e.matmul(o, l, r, start=s, stop=t), reads, writes)

        dma("sp", gall[:], gall_d, [], [("c", "gall")])
        dma("sp", cconv[:], cconv_d, [], [("c", "cconv")])
        dma("sp", bconv[:], bconv_d, [], [("c", "bconv")])
        dma("sp", cvec[:], cvec_d, [], [("c", "cvec")])
        dma("sp", cmat[:], cmat_d, [], [("c", "cmat")])
        dma("sp", hmask[:], hmask_d, [], [("c", "hmask")])
        dma("sp", pcorr[:], pcorr_d, [], [("c", "pcorr")])
        dma("pool", triu[:], triu_d, [], [("c", "triu")])
        dve(lambda e: e.memset(onesb[:], 1.0), [], [("c", "ones")])
        dve(lambda e: e.memset(zeros[:], 0.0), [], [("c", "zeros")])
        dve(lambda e: e.memset(tmp[:], 0.0), [], [("tmp", i) for i in range(NTMP)])
        dve(lambda e: e.memset(dbf[:], 0.0), [("dbf", 0), ("dbf", 1)], [("dbf", 0), ("dbf", 1)])

        wstate = {"slot": 0}

        wflat = wring[:, :, :].rearrange("p a b -> p (a b)")

        def wload(Wv, kc0, nk, c0, ncol):
            size = nk * ncol
            nsub = size // 2048
            j = wstate["slot"]
            if nsub == 2 and j % 2 == 1:
                j = (j + 1) % 6
            wstate["slot"] = (j + nsub) % 6
            view = wflat[:, j * 2048:j * 2048 + size].rearrange("p (k c) -> p k c", k=nk)
            regs = [("w", j + t) for t in range(nsub)]
            dma("pool", view, Wv[:, kc0:kc0 + nk, c0:c0 + ncol], [], regs, chan=("w", j))
            return regs, view

        pending = []
        epoch = {"e": 0}

        def push_job(fn):
            pending.append((epoch["e"], fn))

        def run_pending(n):
            for _ in range(n):
                if pending and pending[0][0] < epoch["e"]:
                    pending.pop(0)[1]()
            epoch["e"] += 1

        def flush_pending():
            while pending:
                pending.pop(0)[1]()

        def ntiles(t0, t1):
            n = (t1 - t0) // 2
            return [(t0, t0 + n), (t0 + n, t1)]

        def stream_fm(Wv, kchunks, col0, ncols, KB, CB, rhs_fn, rhs_reg, tiles, evac, bank0=0, kborder=None, npop=1):
            nkb = len(kchunks) // KB
            nct = CB // 128
            kbs = list(range(nkb)) if kborder is None else kborder
            for cb in range(ncols // CB):
                c0 = col0 + cb * CB
                for bi, kb in enumerate(kbs):
                    regs, view = wload(Wv, kchunks[kb * KB], KB, c0, CB)
                    for ct in range(nct):
                        for kk in range(KB):
                            kidx = kb * KB + kk
                            for ti, (a, b) in enumerate(tiles):
                                bank = bank0 + ct * len(tiles) + ti
                                MM(ps[:, bank, 0:b - a], view[:, kk, ct * 128:(ct + 1) * 128], rhs_fn(kidx, a, b),
                                   bi == 0 and kk == 0, bi == nkb - 1 and kk == KB - 1,
                                   regs + [rhs_reg(kidx)], [("ps", bank)])
                    if bi < nkb - 1:
                        run_pending(npop)
                for ct in range(nct):
                    banks = [bank0 + ct * len(tiles) + ti for ti in range(len(tiles))]
                    evac((c0 - col0) // 128 + ct, banks)
                run_pending(npop)

        def norm_sq(ch, xap, xregion, ntok, slot, defer, first_of_buf):
            tl = ntiles(0, ntok) if ntok > 512 else [(0, ntok)]
            bi = slot // 2
            sqv = tmp[:, bi, :].bitcast(BF16)
            o_ = (slot % 2) * LPAD
            ACT(sqv[:, o_:o_ + ntok], xap, AF.Square, [xregion, ("tmp", bi)],
                [("sqs", slot)] + ([("tmp", bi)] if first_of_buf else []))

            def job(ch=ch):
                for ti, (a, b) in enumerate(tl):
                    MM(ps[:, 4 + ti, 0:b - a], onesb[:, :], sqv[:, o_ + a:o_ + b], ch == 0, ch == NCH - 1,
                       [("sqs", slot), ("tmp", bi), ("c", "ones")], [("ps", 4 + ti)])
            if defer:
                push_job(job)
            else:
                job()

        def norm_fin(xsrc, xreg, ntok, gi, hdst, hreg):
            flush_pending()
            tl = ntiles(0, ntok) if ntok > 512 else [(0, ntok)]
            for ti, (a, b) in enumerate(tl):
                ACT(rstd[:, a:b], ps[:, 4 + ti, 0:b - a], AF.Sqrt, [("ps", 4 + ti)], [("rstd", ti)],
                    bias=epsb[:, 0:1], scale=1.0 / 4096.0)
                RECIP(rstd[:, a:b], [("rstd", ti)], [("rstd", ti)])
            for ch in range(NCH):
                STT(hdst(ch), xsrc(ch), gall[:, gi, ch:ch + 1], rstd[:, 0:ntok], ALU.mult, ALU.mult,
                    [xreg(ch), ("c", "gall"), ("rstd", 0), ("rstd", 1)], [hreg(ch)])

        def rmsnorm(xsrc, xreg, ntok, gi, hdst, hreg):
            for ch in range(NCH):
                norm_sq(ch, xsrc(ch), xreg(ch), ntok, 10 + ch % 2, False, ch == 0)
            norm_fin(xsrc, xreg, ntok, gi, hdst, hreg)

        epsb = sb("epsb", [P, 1], F32)
        dve(lambda e: e.memset(epsb[:], EPS), [], [("c", "eps")])

        def run_pass(pi):
            if pi == 0:
                TH, XOFF = 640, 96
                segs_all = [(0, 640, 30)]
                nchunks = 5
            else:
                TH, XOFF = 576, 0
                segs_all = [(0, 512, 30), (512, 544, 572), (544, 576, 634)]
                nchunks = 4
            TX = TH - XOFF
            L = 670 if pi == 0 else 666
            Lc = L - 30

            if pi == 0:
                xfar = tmp[:, 0:5, :].rearrange("p a b -> p (a b)")[:, 0:3072].rearrange("p (c t) -> p c t", c=NCH)
                dma("sp", xfar, xa_d[:, :, 0:96], [], [("tmp", i) for i in range(5)])
                for q in range(4):
                    dma("sp", x[:, q * 8:(q + 1) * 8, 0:544], xa_d[:, q * 8:(q + 1) * 8, 96:640],
                        [], [("x", c) for c in range(q * 8, q * 8 + 8)])
            else:
                for q in range(4):
                    dma("sp", x[:, q * 8:(q + 1) * 8, 0:576], xb_d[:, q * 8:(q + 1) * 8, :],
                        [], [("x", c) for c in range(q * 8, q * 8 + 8)])

            for l in range(DEPTH):
                t0 = 0 if (pi == 1 or l == 0) else 96
                segs = [(max(ts, t0), te, pp + max(ts, t0) - ts) for (ts, te, pp) in segs_all]
                wtiles = ntiles(t0, TH)

                dma("pool", wmT[:], wsT_d[:, l], [], [("c", "wmT")])
                dve(lambda e: [e.tensor_tensor(out=wmT[:, hh, :], in0=wmT[:, hh, :], in1=triu[:, :], op=ALU.mult)
                               for hh in range(8)][-1], [("c", "wmT"), ("c", "triu")], [("c", "wmT")])
                dma("sp", abT[:], ab_d[:, l], [], [("c", "abT")])
                if pi == 1:
                    dma("sp", stc[:], stc_d[:, l], [], [("c", "stc")])
                    dma("sp", stb[:], stb_d[:, l], [], [("c", "stb")])
                    dma("sp", std[:], std_d[:, l], [], [("c", "std")])
                dma("pool", dwb[:], dwT_d[:, l], [], [("c", "dwb")])

                MM(ps[:, 7, 0:8], cmat[:, 0, :], cvec[:, l, 0, :], True, True, [("c", "cmat"), ("c", "cvec")], [("ps", 7)])
                ACT(cbc[:, :], ps[:, 7, 0:8], AF.Copy, [("ps", 7)], [("c", "cbc")])

                if pi == 0 and l == 0:
                    rmsnorm(lambda ch: xfar[:, ch, :], lambda ch: ("tmp", (ch * 96) // LPAD), 96, 0,
                            lambda ch: h[:, ch, 0:96], lambda ch: ("h", ch))
                if l == 0:
                    rmsnorm(lambda ch: x[:, ch, 0:TX], lambda ch: ("x", ch), TX, 2 * l,
                            lambda ch: h[:, ch, XOFF:TH], lambda ch: ("h", ch))
                else:
                    norm_fin(lambda ch: x[:, ch, 0:TX], lambda ch: ("x", ch), TX, 2 * l,
                             lambda ch: h[:, ch, XOFF:TH], lambda ch: ("h", ch))

                Win = w_in_d[l].rearrange("(kc p) n -> p kc n", p=P)
                Wout = w_out_d[l].rearrange("(kc p) n -> p kc n", p=P)
                W1 = w_ff1_d[l].rearrange("(kc p) n -> p kc n", p=P)
                W2 = w_ff2_d[l].rearrange("(kc p) n -> p kc n", p=P)
                allk = list(range(NCH))
                hrhs = lambda k, a, b: h[:, k, a:b]
                hreg = lambda k: ("h", k)

                def seg_pieces(a, b):
                    out = []
                    for (ts, te, pp) in segs:
                        lo, hi = max(a, ts), min(b, te)
                        if lo < hi:
                            out.append((lo, hi, pp + lo - ts))
                    return out

                def evac_to_pad(banks, dst_i, func, extra_reads=()):
                    for bank, (a, b) in zip(banks, wtiles):
                        for (lo, hi, pl) in seg_pieces(a, b):
                            ACT(tmp[:, dst_i, pl:pl + hi - lo], ps[:, bank, lo - a:hi - a], func,
                                [("ps", bank)] + list(extra_reads), [("tmp", dst_i)])

                def fill_hist(dst_i, nh, tl_t, st_t, ch):
                    if pi == 0:
                        ACOPY(tmp[:, dst_i, 30 - nh:30], zeros[:, 0:nh], [("c", "zeros")], [("tmp", dst_i)])
                    else:
                        ACOPY(tmp[:, dst_i, 30 - nh:30], tl_t[:, l, ch, :], [("tl", nh, l, ch)], [("tmp", dst_i)])
                        for sq_ in range(2):
                            pp = segs_all[1 + sq_][2]
                            ACOPY(tmp[:, dst_i, pp - nh:pp], st_t[:, sq_, ch, :],
                                  [("c", "stc"), ("c", "stb"), ("c", "std")], [("tmp", dst_i)])

                def save_tails(src_i, nh, tl_t, out_d, ch):
                    if pi == 0:
                        ACOPY(tl_t[:, l, ch, :], tmp[:, src_i, 670 - nh:670], [("tmp", src_i)], [("tl", nh, l, ch)])
                    else:
                        for si, (ts, te, pp) in enumerate(segs_all):
                            pe_ = pp + te - ts
                            dma("sp", out_d[:, l, si, ch, :], tmp[:, src_i, pe_ - nh:pe_], [("tmp", src_i)], [])

                def pad_to_mix(src_ap_fn, mt, emit):
                    for (ts, te, pp) in segs_all:
                        lo = max(ts, XOFF)
                        if lo >= te:
                            continue
                        j0 = pp + (lo - ts) - 30
                        emit(mix[:, mt, lo - XOFF:te - XOFF], src_ap_fn(j0, j0 + te - lo))

                for i in range(4):
                    def evac_cg(ct, banks, i=i):
                        p_ = ct % 2
                        evac_to_pad(banks, p_, AF.Sigmoid)
                    stream_fm(Win, allk, 6144 + 256 * i, 256, 8, 256, hrhs, hreg, wtiles, evac_cg)

                    def evac_ca(ct, banks, i=i):
                        p_ = ct % 2
                        ch = 2 * i + p_
                        S, G, Cn, Sq = p_, 2, 3 + p_, 5
                        if p_ == 0:
                            flush_pending()
                        evac_to_pad(banks, G, AF.Copy)
                        TT(tmp[:, G, 30:L], tmp[:, G, 30:L], tmp[:, S, 30:L], ALU.mult, [("tmp", G), ("tmp", S)], [("tmp", G)])
                        fill_hist(G, 30, tlc, stc, ch)
                        save_tails(G, 30, tlc, oc_d, ch)
                        dve(lambda e, o=glub[:, p_, 0:L], a=tmp[:, G, 0:L]: e.tensor_copy(out=o, in_=a), [("tmp", G)], [("glub", p_)])
                        ct2 = ntiles(0, Lc)

                        def build_dg(tg, ch=ch):
                            for j, k in enumerate(range(6 * tg, min(6 * tg + 6, 31))):
                                TS(dg[:, tg % 2, j, :], cmat[:, 0, :], cconv[:, l, ch, k:k + 1], None, ALU.mult, None,
                                   [("c", "cmat"), ("c", "cconv")], [("dg", tg % 2)])

                        def job_diag():
                            build_dg(0)
                            build_dg(1)

                        def job_conv(ch=ch, p_=p_):
                            for tg in range(6):
                                taps = list(range(6 * tg, min(6 * tg + 6, 31)))
                                if tg >= 2:
                                    build_dg(tg)
                                for j, k in enumerate(taps):
                                    for ti, (a, b) in enumerate(ct2):
                                        MM(ps[:, 4 + ti, 0:b - a], dg[:, tg % 2, j, :], glub[:, p_, k + a:k + b], k == 0, k == 30,
                                           [("dg", tg % 2), ("glub", p_)], [("ps", 4 + ti)])
                            for ti, (a, b) in enumerate(ct2):
                                ACT(tmp[:, Cn, a:b], ps[:, 4 + ti, 0:b - a], AF.Identity, [("ps", 4 + ti), ("c", "cbc")], [("tmp", Cn)],
                                    bias=cbc[:, ch:ch + 1])
                                ACT(tmp[:, Sq, a:b], ps[:, 4 + ti, 0:b - a], AF.Square, [("ps", 4 + ti), ("c", "cbc")], [("tmp", Sq)],
                                    bias=cbc[:, ch:ch + 1])

                        def job_var(ch=ch, p_=p_):
                            for ti, (a, b) in enumerate(ct2):
                                MM(ps[:, 6 + ti, 0:b - a], cmat[:, 1, :], tmp[:, Sq, a:b], True, True,
                                   [("c", "cmat"), ("tmp", Sq)], [("ps", 6 + ti)])
                            for ti, (a, b) in enumerate(ct2):
                                ACT(tmp[:, Sq, a:b], ps[:, 6 + ti, 0:b - a], AF.Sqrt, [("ps", 6 + ti)], [("tmp", Sq)],
                                    bias=epsb[:, 0:1], scale=1.0)
                            RECIP(tmp[:, Sq, 0:Lc], [("tmp", Sq)], [("tmp", Sq)])
                            TT(tmp[:, Cn, 0:Lc], tmp[:, Cn, 0:Lc], tmp[:, Sq, 0:Lc], ALU.mult, [("tmp", Cn), ("tmp", Sq)], [("tmp", Cn)])
                            pad_to_mix(lambda j0, j1: tmp[:, Cn, j0:j1], ch,
                                       lambda mo, so: ACT(mo, so, AF.Silu, [("tmp", Cn), ("c", "cvec")], [("mix", ch)],
                                                          bias=cvec[:, l, 2, ch:ch + 1], scale=cvec[:, l, 1, ch:ch + 1]))
                        push_job(job_diag)
                        push_job(job_conv)
                        push_job(job_var)
                    stream_fm(Win, allk, 5120 + 256 * i, 256, 8, 256, hrhs, hreg, wtiles, evac_ca)

                for g in range(4):
                    def evac_dp(ct, banks, g=g):
                        p_ = ct % 2
                        ch = 2 * g + p_
                        D0, A1, A2 = p_, 2 + 2 * p_, 3 + 2 * p_
                        if g == 0:
                            A1 = A2 = 2
                        elif p_ == 0:
                            flush_pending()
                        evac_to_pad(banks, D0, AF.Copy)
                        fill_hist(D0, 15, tld, std, ch)
                        save_tails(D0, 15, tld, od_d, ch)
                        w = POOLW[g]
                        src, vs, step = D0, 15, 1
                        pp_ = [A1, A2]
                        k = 0
                        while step < w:
                            dst = pp_[k % 2]
                            nv = vs + step
                            TT(tmp[:, dst, nv:L], tmp[:, src, nv:L], tmp[:, src, nv - step:L - step], ALU.add,
                               [("tmp", src)], [("tmp", dst)])
                            src, vs, step, k = dst, nv, step * 2, k + 1
                        if pi == 0:
                            TT(tmp[:, src, 158:174], tmp[:, src, 158:174], pcorr[:, g, :], ALU.mult,
                               [("tmp", src), ("c", "pcorr")], [("tmp", src)])
                        STT(dbf[:, p_, 30:L], tmp[:, src, 30:L], 1.0 / w, tmp[:, D0, 30:L], ALU.mult, ALU.subtract,
                            [("tmp", src), ("tmp", D0)], [("dbf", p_)])
                        def job_dmap(g=g):
                            dtl = ntiles(30, L)
                            for dt_ in range(2):
                                for kc in range(2):
                                    for ti, (a, b) in enumerate(dtl):
                                        bank = 4 + dt_ * 2 + ti
                                        MM(ps[:, bank, 0:b - a], dwb[:, g, kc, dt_ * 128:(dt_ + 1) * 128], dbf[:, kc, a:b],
                                           kc == 0, kc == 1, [("c", "dwb"), ("dbf", kc)], [("ps", bank)])
                            for dt_ in range(2):
                                chd = 2 * g + dt_
                                for ti, (a, b) in enumerate(dtl):
                                    bank = 4 + dt_ * 2 + ti
                                    for (ts, te, pp) in segs_all:
                                        lo_t = max(ts, XOFF)
                                        if lo_t >= te:
                                            continue
                                        plo, phi = pp + lo_t - ts, pp + te - ts
                                        qlo, qhi = max(plo, a), min(phi, b)
                                        if qlo >= qhi:
                                            continue
                                        tlo = lo_t + (qlo - plo)
                                        ACT(mix[:, 8 + chd, tlo - XOFF:tlo - XOFF + qhi - qlo], ps[:, bank, qlo - a:qhi - a],
                                            AF.Copy, [("ps", bank), ("c", "cvec")], [("mix", 8 + chd)],
                                            scale=cvec[:, l, 3, chd:chd + 1])
                        if p_ == 1:
                            push_job(job_dmap)
                    stream_fm(Win, allk, 7168 + 256 * g, 256, 8, 256, hrhs, hreg, wtiles, evac_dp, npop=2)
                def b_first(i):
                    def evac_bh(ct, banks):
                        evac_to_pad(banks, 2 * (ct % 2), AF.Copy)
                    stream_fm(Win, allk, 2048 + 256 * i, 256, 8, 256, hrhs, hreg, wtiles, evac_bh)

                    def evac_bc(ct, banks, i=i):
                        p_ = ct % 2
                        ch = 2 * i + p_
                        H, Cb = 2 * p_, 2 * p_ + 1
                        evac_to_pad(banks, Cb, AF.Copy)
                        TT(tmp[:, Cb, 30:L], tmp[:, Cb, 30:L], tmp[:, H, 30:L], ALU.mult, [("tmp", Cb), ("tmp", H)], [("tmp", Cb)])
                        fill_hist(Cb, 2, tlb, stb, ch)
                        save_tails(Cb, 2, tlb, ob_d, ch)
                        TS(tmp[:, H, 0:Lc], tmp[:, Cb, 28:28 + Lc], bconv[:, l, ch, 0:1], None, ALU.mult, None,
                           [("tmp", Cb), ("c", "bconv")], [("tmp", H)])
                        for k in (1, 2):
                            STT(tmp[:, H, 0:Lc], tmp[:, Cb, 28 + k:28 + k + Lc], bconv[:, l, ch, k:k + 1], tmp[:, H, 0:Lc],
                                ALU.mult, ALU.add, [("tmp", Cb), ("tmp", H), ("c", "bconv")], [("tmp", H)])
                    stream_fm(Win, allk, 4096 + 256 * i, 256, 8, 256, hrhs, hreg, wtiles, evac_bc)

                def b_second(i):
                    def evac_bb(ct, banks, i=i):
                        p_ = ct % 2
                        ch = 2 * i + p_
                        H, Cb = 2 * p_, 2 * p_ + 1
                        evac_to_pad(banks, Cb, AF.Copy)
                        for (ts, te, pp) in segs_all:
                            lo = max(ts, XOFF)
                            if lo >= te:
                                continue
                            p0 = pp + lo - ts
                            n = te - lo
                            TT(mix[:, 8 + ch, lo - XOFF:te - XOFF], tmp[:, Cb, p0:p0 + n], tmp[:, H, p0 - 30:p0 - 30 + n], ALU.mult,
                               [("tmp", Cb), ("tmp", H)], [("mix", 8 + ch)])
                    stream_fm(Win, allk, 3072 + 256 * i, 256, 8, 256, hrhs, hreg, wtiles, evac_bb)

                b_first(0)
                flush_pending()
                xtiles = ntiles(32 if (pi == 0 and l == 1) else 0, TX)

                def evac_x(ct, banks):
                    for bank, (a, b) in zip(banks, xtiles):
                        TT(x[:, ct, a:b], ps[:, bank, 0:b - a], x[:, ct, a:b], ALU.add, [("ps", bank), ("x", ct)], [("x", ct)])

                stream_fm(Wout, list(range(16, 32)), 0, 4096, 8, 256, lambda k, a, b: mix[:, k, a:b],
                          lambda k: ("mix", k), xtiles, evac_x)

                b_second(0)
                for i in range(1, 4):
                    b_first(i)
                    b_second(i)

                for i in range(4):
                    c0 = 1024 + 256 * i
                    views = [wload(Win, 8 * q_, 8, c0, 256) for q_ in range(4)]
                    jobs = [(tc * 128, tc * 128 + 128, tc, None) for tc in range(nchunks)]
                    if pi == 1:
                        jobs += [(512, 544, None, 0), (544, 576, None, 1)]
                    avbanks = [0, 1, 2, 3, 6, 7]
                    for kq in range(4):
                        regs_, v_ = views[kq]
                        for ji, (a, b, tc, sq_) in enumerate(jobs):
                            bank = avbanks[ji]
                            m = b - a
                            for kk in range(8):
                                kc = kq * 8 + kk
                                MM(ps[0:m, bank, 0:256], h[:, kc, a:b], v_[:, kk, :], kc == 0, kc == NCH - 1,
                                   regs_ + [("h", kc)], [("ps", bank)])
                        if kq == 0:
                            flush_pending()
                    for ji, (a, b, tc, sq_) in enumerate(jobs):
                        bank = avbanks[ji]
                        if tc is not None:
                            ACT(vtok[:, tc, :], ps[:, bank, 0:256], AF.Gelu, [("ps", bank)], [("vtok", tc)])
                        else:
                            ACT(avs[0:32, sq_, :], ps[0:32, bank, 0:256], AF.Gelu, [("ps", bank)], [("avs", sq_)])
                            ACOPY(vtoks[0:32, sq_, :], avs[0:32, sq_, :], [("avs", sq_)], [("vtoks", sq_)])
                            dma("sp", oav_d[0:32, l, sq_, 256 * i:256 * i + 256], avs[0:32, sq_, :], [("avs", sq_)], [])

                    def evac_au(ct, banks, i=i):
                        hd = 2 * i + ct
                        for bank, (a, b) in zip(banks, wtiles):
                            lo = max(a, XOFF)
                            if lo < b:
                                ACT(mix[:, hd, lo - XOFF:b - XOFF], ps[:, bank, lo - a:b - a], AF.Gelu, [("ps", bank)], [("mix", hd)])
                    stream_fm(Win, allk, 256 * i, 256, 8, 256, hrhs, hreg, wtiles, evac_au)

                    def job_gate(i=i):
                        for hh in range(2):
                            hd = 2 * i + hh
                            gj = []
                            for tc in range(nchunks):
                                bank, cc = (4, tc * 128) if tc < 4 else (5, 0)
                                MM(ps[:, bank, cc:cc + 128], vtok[:, tc, hh * 128:(hh + 1) * 128], wmT[:, hd, :], True, True,
                                   [("vtok", tc), ("c", "wmT")], [("ps", bank)])
                                gj.append((bank, cc, 128, tc * 128, 0))
                            if pi == 1:
                                for sq_ in range(2):
                                    MM(ps[:, 5, 32 * sq_:32 * sq_ + 32], vtoks[0:32, sq_, hh * 128:(hh + 1) * 128], wmT[0:32, hd, 0:32],
                                       True, True, [("vtoks", sq_), ("c", "wmT")], [("ps", 5)])
                                    gj.append((5, 32 * sq_, 32, 512 + 32 * sq_, 0))
                            for (bank, cc, n, tlo, bl) in gj:
                                lo = max(tlo, XOFF)
                                if lo >= tlo + n:
                                    continue
                                sk = lo - tlo
                                n2 = n - sk
                                gt = tmp[:, 4 + hh, 0:n2]
                                TT(gt, ps[:, bank, cc + sk:cc + n], abT[:, hd, bl + sk:bl + n], ALU.add,
                                   [("ps", bank), ("c", "abT")], [("tmp", 4 + hh)])
                                TT(mix[:, hd, lo - XOFF:lo - XOFF + n2], gt, mix[:, hd, lo - XOFF:lo - XOFF + n2], ALU.mult,
                                   [("tmp", 4 + hh), ("mix", hd)], [("mix", hd)])
                    push_job(job_gate)
                flush_pending()

                def evac_x_stats(ct, banks, mask=False):
                    evac_x(ct, banks)
                    if mask:
                        TS(x[:, ct, 0:32], x[:, ct, 0:32], hmask[:, 0:1], None, ALU.mult, None,
                           [("x", ct), ("c", "hmask")], [("x", ct)])
                    norm_sq(ct, x[:, ct, 0:TX], ("x", ct), TX, ct % 12, True, ct < 12 and ct % 2 == 0)

                stream_fm(Wout, list(range(0, 16)), 0, 4096, 8, 256, lambda k, a, b: mix[:, k, a:b],
                          lambda k: ("mix", k), xtiles, evac_x_stats, kborder=[1, 0])

                norm_fin(lambda ch: x[:, ch, 0:TX], lambda ch: ("x", ch), TX, 2 * l + 1,
                         lambda ch: h[:, ch, XOFF:TH], lambda ch: ("h", ch))
                hx = lambda k, a, b: h[:, k, XOFF + a:XOFF + b]
                rot = {"i": 0}
                for grp in range(8):
                    def evac_f1(ct, banks, grp=grp):
                        for bank, (a, b) in zip(banks, xtiles):
                            ti_ = rot["i"] % 4
                            rot["i"] += 1
                            ACT(tmp[:, ti_, 0:b - a], ps[:, bank, 0:b - a], AF.Relu, [("ps", bank)], [("tmp", ti_)])
                            TT(mix[:, ct, a:b], tmp[:, ti_, 0:b - a], tmp[:, ti_, 0:b - a], ALU.mult, [("tmp", ti_)], [("mix", ct)])
                    stream_fm(W1, allk, grp * 2048, 2048, 8, 512, hx, hreg, xtiles, evac_f1)
                    if grp < 7:
                        stream_fm(W2, list(range(grp * 16, grp * 16 + 16)), 0, 4096, 8, 512, lambda k, a, b: mix[:, k, a:b],
                                  lambda k: ("mix", k), xtiles, evac_x)
                    else:
                        stream_fm(W2, list(range(grp * 16, grp * 16 + 16)), 0, 4096, 8, 256, lambda k, a, b: mix[:, k, a:b],
                                  lambda k: ("mix", k), xtiles,
                                  lambda ct, banks: evac_x_stats(ct, banks, mask=(pi == 0 and l == 0)))

            norm_fin(lambda ch: x[:, ch, 0:TX], lambda ch: ("x", ch), TX, 4,
                     lambda ch: x[:, ch, 0:TX], lambda ch: ("x", ch))
            for q in range(4):
                regs = [("x", c) for c in range(q * 8, q * 8 + 8)]
                if pi == 0:
                    dma("sp", ya_d[:, q * 8:(q + 1) * 8, :], x[:, q * 8:(q + 1) * 8, 32:544], regs, [])
                else:
                    dma("sp", yb_d[:, q * 8:(q + 1) * 8, :], x[:, q * 8:(q + 1) * 8, 0:576], regs, [])

        run_pass(0)
        run_pass(1)

        sem_names = set()
        for e in Rec.ENGS:
            cnt = 0
            for op in R.ops[e]:
                if op.is_dma:
                    sem_names.add(op.sem)
                elif op.flagged:
                    cnt += 1
                    op.val = cnt
                    op.sem = ("prog", e)
            sem_names.add(("prog", e))
        sem_names = sorted(sem_names, key=str)
        sems = {}
        for i, k in enumerate(sem_names):
            sems[k] = es.enter_context(nc.semaphore("s%d" % i))

        out_dmas = [op for e in Rec.ENGS for op in R.ops[e] if op.is_dma]

        def emit(engobj, ename):
            waited = {}
            for op in R.ops[ename]:
                need = {}
                for d in op.deps:
                    if d.val is None:
                        continue
                    if need.get(d.sem, 0) < d.val:
                        need[d.sem] = d.val
                for k, v in need.items():
                    if waited.get(k, 0) < v:
                        engobj.wait_ge(sems[k], v)
                        waited[k] = v
                ins = op.fn(engobj)
                if op.is_dma:
                    ins.then_inc(sems[op.sem], 16)
                elif op.flagged:
                    ins.then_inc(sems[op.sem], 1)
            if ename == "sp":
                fin = {}
                for op in out_dmas:
                    if fin.get(op.sem, 0) < op.val:
                        fin[op.sem] = op.val
                for k, v in fin.items():
                    engobj.wait_ge(sems[k], v)

        with nc.Block() as block:
            @block.tensor
            def _(e):
                emit(e, "pe")

            @block.scalar
            def _(e):
                emit(e, "act")

            @block.vector
            def _(e):
                emit(e, "dve")

            @block.gpsimd
            def _(e):
                emit(e, "pool")

            @block.sync
            def _(e):
                emit(e, "sp")
    return nc


def _fm(a):
    T, C = a.shape
    return np.ascontiguousarray(a.reshape(T, C // 128, 128).transpose(2, 1, 0))


def _fm_inv(a):
    p, n, T = a.shape
    return np.ascontiguousarray(a.transpose(2, 1, 0).reshape(T, n * 128))


_NC_CACHE = {}


def kernel(x_prompt, x_sample, state_conv_b, state_conv_c, state_pool, g_mix, w_in, a_ws, a_b, b_conv,
           c_conv, c_conv_b, c_ln_g, c_ln_b, d_w, d_scale, w_out, g_ffn, w_ff1, w_ff2, g_final):
    f = lambda a: np.ascontiguousarray(np.asarray(a, dtype=np.float32))
    x_prompt, x_sample = f(x_prompt), f(x_sample)
    state_conv_b, state_conv_c, state_pool = f(state_conv_b), f(state_conv_c), f(state_pool)
    w_in, w_out, w_ff1, w_ff2 = f(w_in), f(w_out), f(w_ff1), f(w_ff2)
    g_mix, g_ffn, g_final = f(g_mix), f(g_ffn), f(g_final)
    a_ws, a_b, b_conv, c_conv = f(a_ws), f(a_b), f(b_conv), f(c_conv)
    c_conv_b, c_ln_g, c_ln_b, d_w, d_scale = f(c_conv_b), f(c_ln_g), f(c_ln_b), f(d_w), f(d_scale)

    def vec_fm(v, nchunk):
        return v.reshape(nchunk, 128).T

    g_all = np.ascontiguousarray(np.stack([vec_fm(g_mix[0], 32), vec_fm(g_ffn[0], 32), vec_fm(g_mix[1], 32),
                                           vec_fm(g_ffn[1], 32), vec_fm(g_final, 32)], axis=1))
    cconv = np.ascontiguousarray(c_conv.reshape(2, 31, 8, 128).transpose(3, 0, 2, 1))
    bconv = np.ascontiguousarray(b_conv.reshape(2, 3, 8, 128).transpose(3, 0, 2, 1))
    cvec = np.ascontiguousarray(np.stack([c_conv_b, c_ln_g, c_ln_b, d_scale], axis=1).reshape(2, 4, 8, 128).transpose(3, 0, 1, 2))
    wsT = np.ascontiguousarray(a_ws.transpose(3, 0, 1, 2))
    jj, ii = np.meshgrid(np.arange(128), np.arange(128), indexing="ij")
    triu = np.ascontiguousarray((jj <= ii).astype(np.float32))
    ab = np.ascontiguousarray(np.broadcast_to(a_b[None], (128, 2, 8, 128)))
    dwT = np.ascontiguousarray(d_w.reshape(2, 4, 2, 128, 256).transpose(3, 0, 1, 2, 4))
    cm = np.stack([np.eye(128, dtype=np.float32) - 1.0 / 128.0, np.full((128, 128), 1.0 / 128.0, np.float32)], axis=1)
    cmat = np.ascontiguousarray(cm.astype(np.float32))

    in_maps = []
    for c in range(8):
        b, k = c // 4, c % 4
        s = 1024 * k
        xa = np.zeros((640, 4096), np.float32)
        if k > 0:
            xa[:] = x_prompt[b, s - 128:s + 512]
        else:
            xa[128:] = x_prompt[b, 0:512]
        xb = np.concatenate([x_prompt[b, s + 512:s + 1024], x_sample[2 * c], x_sample[2 * c + 1]], axis=0)
        pc = np.ones((128, 4, 16), np.float32)
        if k == 0:
            for g, w in enumerate(POOLW):
                for t in range(16):
                    pc[:, g, t] = np.float32(w) / np.float32(min(t + 1, w))
        sq = [2 * c, 2 * c + 1]

        def st_fm(st, nr):
            a = st[:, sq]
            return np.ascontiguousarray(a.reshape(2, 2, nr, 8, 128).transpose(4, 0, 1, 3, 2))
        in_maps.append({
            "xa": _fm(xa), "xb": _fm(xb),
            "hmask": np.full((128, 1), 0.0 if k == 0 else 1.0, np.float32),
            "pcorr": pc,
            "st_b": st_fm(state_conv_b, 2), "st_c": st_fm(state_conv_c, 30), "st_d": st_fm(state_pool, 15),
            "g_all": g_all, "cconv": cconv, "bconv": bconv, "cvec": cvec, "wsT": wsT, "triu": triu, "ab": ab,
            "dwT": dwT, "cmat": cmat, "w_in": w_in, "w_out": w_out, "w_ff1": w_ff1, "w_ff2": w_ff2,
        })

    if "nc" not in _NC_CACHE:
        _NC_CACHE["nc"] = build_program()
    nc = _NC_CACHE["nc"]
    res = run_bass_kernel_spmd(nc, in_maps, core_ids=list(range(8)))
    outs = res.results

    y_prompt = np.zeros((2, 4096, 4096), np.float32)
    y_sample = np.zeros((16, 32, 4096), np.float32)
    ncb_p = np.zeros((2, 2, 2, 1024), np.float32)
    ncc_p = np.zeros((2, 2, 30, 1024), np.float32)
    npl_p = np.zeros((2, 2, 15, 1024), np.float32)
    ncb_s = np.zeros((2, 16, 2, 1024), np.float32)
    ncc_s = np.zeros((2, 16, 30, 1024), np.float32)
    npl_s = np.zeros((2, 16, 15, 1024), np.float32)
    nav_s = np.zeros((2, 16, 32, 1024), np.float32)

    def tail_inv(a):
        return a.transpose(2, 1, 0).reshape(a.shape[2], 1024)

    for c in range(8):
        b, k = c // 4, c % 4
        s = 1024 * k
        o = outs[c]
        y_prompt[b, s:s + 512] = _fm_inv(o["ya"])
        yb = _fm_inv(o["yb"])
        y_prompt[b, s + 512:s + 1024] = yb[0:512]
        y_sample[2 * c] = yb[512:544]
        y_sample[2 * c + 1] = yb[544:576]
        for l in range(2):
            for si in range(3):
                tb, tc_, td = tail_inv(o["ob"][:, l, si]), tail_inv(o["oc"][:, l, si]), tail_inv(o["od"][:, l, si])
                if si == 0:
                    if k == 3:
                        ncb_p[l, b], ncc_p[l, b], npl_p[l, b] = tb, tc_, td
                else:
                    q = 2 * c + si - 1
                    ncb_s[l, q], ncc_s[l, q], npl_s[l, q] = tb, tc_, td
            for si in range(2):
                nav_s[l, 2 * c + si] = o["oav"][:, l, si, :]
    return (y_prompt, y_sample, ncb_p, ncc_p, npl_p, ncb_s, ncc_s, npl_s, nav_s)
```
